# Optimizing a Trainium2 kernel written in Bass

```python
import math
import jax, jax.numpy as jnp
from jax import lax
import numpy as np

D_MODEL = 1024
BATCH = 4
SEQ = 4096
DEPTH = 4

GRID_W = 64
CTX_LEN = 256
W_BR = 512
CONV_K = 31
CHUNK = 128
SGU_HEADS = 8
SSM_GROUP = 16
SSM_GROUPS = W_BR // SSM_GROUP
SSM_STATE = 64
N_DIR = 2
D_FF = (8 * D_MODEL + 3 * 256 - 1) // (3 * 256) * 256
N_IN = 5 * W_BR + 3 * D_MODEL
EPS = 1e-6

kernel_name = "hybrid_conv_gmlp_s5_dit_block"


def _rmsnorm(x, g):
    x32 = x.astype(jnp.float32)
    y = x32 * lax.rsqrt(jnp.mean(x32 * x32, axis=-1, keepdims=True) + EPS)
    return y.astype(x.dtype) * g


def _layernorm(x, g, b):
    x32 = x.astype(jnp.float32)
    mu = jnp.mean(x32, axis=-1, keepdims=True)
    var = jnp.mean(jnp.square(x32 - mu), axis=-1, keepdims=True)
    return ((x32 - mu) * lax.rsqrt(var + EPS)).astype(x.dtype) * g + b


def _sincos_1d(pos, dim):
    half = dim // 2
    omega = 1.0 / (10000.0 ** (jnp.arange(half, dtype=jnp.float32) / half))
    ang = pos[:, None] * omega[None, :]
    return jnp.concatenate([jnp.sin(ang), jnp.cos(ang)], axis=-1)


def _grid_pos_embed(rows, dim):
    emb_r = _sincos_1d(jnp.arange(rows, dtype=jnp.float32), dim // 2)
    emb_c = _sincos_1d(jnp.arange(GRID_W, dtype=jnp.float32), dim // 2)
    pe = jnp.concatenate([
        jnp.broadcast_to(emb_r[:, None, :], (rows, GRID_W, dim // 2)),
        jnp.broadcast_to(emb_c[None, :, :], (rows, GRID_W, dim // 2))], axis=-1)
    return pe.reshape(rows * GRID_W, dim)


def _modulate(h, shift, scale):
    return h * (1.0 + scale) + shift


def _dwconv(x, w, b):
    y = lax.conv_general_dilated(
        x, w[:, None, :], window_strides=(1,), padding=[(CONV_K // 2, CONV_K // 2)],
        dimension_numbers=('NWC', 'WIO', 'NWC'), feature_group_count=x.shape[-1])
    return y + b


def _chunk_spatial(v, w_s, b_s):
    n_b, n_l, n_c = v.shape
    v = v.reshape(n_b, n_l // CHUNK, CHUNK, SGU_HEADS, n_c // SGU_HEADS)
    y = jnp.einsum('hpq,bnqhc->bnphc', w_s, v) + b_s.T[None, None, :, :, None]
    return y.reshape(n_b, n_l, n_c)


def _complex_affine_combine(e1, e2):
    a1r, a1i, b1r, b1i = e1
    a2r, a2i, b2r, b2i = e2
    return (a1r * a2r - a1i * a2i,
            a1r * a2i + a1i * a2r,
            a2r * b1r - a2i * b1i + b2r,
            a2r * b1i + a2i * b1r + b2i)


def _ssm_states(u, h0, lam_re, lam_im, log_dt, b_re, b_im):
    lam_re, lam_im, log_dt, b_re, b_im = (t.astype(jnp.float32) for t in (lam_re, lam_im, log_dt, b_re, b_im))
    dt = jnp.exp(log_dt)[:, None]
    a_mag = jnp.exp(lam_re * dt)
    ang = lam_im * dt
    a_re, a_im = a_mag * jnp.cos(ang), a_mag * jnp.sin(ang)
    den = lam_re * lam_re + lam_im * lam_im
    f_re = ((a_re - 1.0) * lam_re + a_im * lam_im) / den
    f_im = (a_im * lam_re - (a_re - 1.0) * lam_im) / den
    bb_re = f_re[..., None] * b_re - f_im[..., None] * b_im
    bb_im = f_re[..., None] * b_im + f_im[..., None] * b_re
    n_b, n_l, _ = u.shape
    ug = u.reshape(n_b, n_l, SSM_GROUPS, SSM_GROUP)
    x_re = jnp.einsum('blgi,gpi->blgp', ug, bb_re)
    x_im = jnp.einsum('blgi,gpi->blgp', ug, bb_im)
    h0_re, h0_im = h0
    x_re = x_re.at[:, 0].add(a_re * h0_re - a_im * h0_im)
    x_im = x_im.at[:, 0].add(a_re * h0_im + a_im * h0_re)
    ar = jnp.broadcast_to(a_re, x_re.shape)
    ai = jnp.broadcast_to(a_im, x_im.shape)
    _, _, h_re, h_im = lax.associative_scan(_complex_affine_combine, (ar, ai, x_re, x_im), axis=1)
    return h_re, h_im


def _ssm_readout(h, c_re, c_im):
    h_re, h_im = h
    y = (jnp.einsum('blgp,gip->blgi', h_re, c_re.astype(jnp.float32))
         - jnp.einsum('blgp,gip->blgi', h_im, c_im.astype(jnp.float32)))
    return y.reshape(y.shape[0], y.shape[1], W_BR)


def _ssm_scan_pair(u, h0_fwd, h0_bwd, lp):
    h_f = _ssm_states(u, h0_fwd, lp['lam_re'][0], lp['lam_im'][0], lp['log_dt'][0], lp['b_re'][0], lp['b_im'][0])
    h_b = _ssm_states(u[:, ::-1], h0_bwd, lp['lam_re'][1], lp['lam_im'][1], lp['log_dt'][1], lp['b_re'][1], lp['b_im'][1])
    return h_f, h_b


def _final(h):
    return (h[0][:, -1], h[1][:, -1])


def _mix_stream(h, h0_fwd, h0_bwd, lp):
    z = h @ lp['w_in'] + lp['b_in']
    za, zb, zs, zg = jnp.split(z, [2 * W_BR, 4 * W_BR, 5 * W_BR], axis=-1)
    a = za[..., :W_BR] * jax.nn.sigmoid(za[..., W_BR:])
    a = _dwconv(a, lp['conv_w'], lp['conv_b'])
    a = jax.nn.silu(_layernorm(a, lp['conv_ln_g'], lp['conv_ln_b']))
    o_a = a @ lp['conv_w_out'] + lp['conv_b_out']
    zb = jax.nn.gelu(zb)
    u_b = zb[..., :W_BR]
    v_b = _layernorm(zb[..., W_BR:], lp['sgu_ln_g'], lp['sgu_ln_b'])
    o_b = (u_b * _chunk_spatial(v_b, lp['sgu_w'], lp['sgu_b'])) @ lp['sgu_w_out'] + lp['sgu_b_out']
    h_f, h_b = _ssm_scan_pair(zs.astype(jnp.float32), h0_fwd, h0_bwd, lp)
    y = (_ssm_readout(h_f, lp['c_re'][0], lp['c_im'][0])
         + _ssm_readout(h_b, lp['c_re'][1], lp['c_im'][1])[:, ::-1]).astype(zs.dtype)
    y = jax.nn.gelu(y + lp['ssm_d'] * zs)
    y = y * jax.nn.sigmoid(y @ lp['ssm_w_glu'] + lp['ssm_b_glu'])
    o_c = y @ lp['ssm_w_out'] + lp['ssm_b_out']
    g_a, g_b, g_c = jnp.split(jax.nn.sigmoid(zg), 3, axis=-1)
    out = (g_a * o_a + g_b * o_b + g_c * o_c) @ lp['w_o'] + lp['b_o']
    return out, h_f, h_b


def _swiglu(h, w_up, w_down):
    g, u = jnp.split(h @ w_up, 2, axis=-1)
    return (jax.nn.silu(g) * u) @ w_down


def setup_inputs(seed: int = 0) -> dict:
    key = jax.random.key(seed)
    ks = jax.random.split(key, 40)
    f32 = jnp.float32
    nrm = lambda k, shape, s: jax.random.normal(k, shape, f32) * s
    D, W, G, P, I = D_MODEL, W_BR, SSM_GROUPS, SSM_STATE, SSM_GROUP
    lam_im0 = jnp.pi * jnp.arange(P, dtype=f32)
    return {
        "x": nrm(ks[0], (BATCH, SEQ, D), 1.0),
        "c": nrm(ks[1], (BATCH, D), 1.0),
        "ctx": nrm(ks[2], (BATCH, CTX_LEN, D), 1.0),
        "c_ctx": nrm(ks[3], (D,), 1.0),
        "w_ada": nrm(ks[4], (DEPTH, D, 6 * D), 0.3 * D ** -0.5),
        "b_ada": nrm(ks[5], (DEPTH, 6 * D), 0.01),
        "norm_g": 1.0 + nrm(ks[6], (DEPTH, 4, D), 0.05),
        "w_in": nrm(ks[7], (DEPTH, D, N_IN), D ** -0.5),
        "b_in": nrm(ks[8], (DEPTH, N_IN), 0.01),
        "conv_w": nrm(ks[9], (DEPTH, CONV_K, W), CONV_K ** -0.5),
        "conv_b": nrm(ks[10], (DEPTH, W), 0.01),
        "conv_ln_g": 1.0 + nrm(ks[11], (DEPTH, W), 0.05),
        "conv_ln_b": nrm(ks[12], (DEPTH, W), 0.01),
        "conv_w_out": nrm(ks[13], (DEPTH, W, D), W ** -0.5),
        "conv_b_out": nrm(ks[14], (DEPTH, D), 0.01),
        "sgu_ln_g": 1.0 + nrm(ks[15], (DEPTH, W), 0.05),
        "sgu_ln_b": nrm(ks[16], (DEPTH, W), 0.01),
        "sgu_w": nrm(ks[17], (DEPTH, SGU_HEADS, CHUNK, CHUNK), 0.5 * CHUNK ** -0.5),
        "sgu_b": 1.0 + nrm(ks[18], (DEPTH, SGU_HEADS, CHUNK), 0.01),
        "sgu_w_out": nrm(ks[19], (DEPTH, W, D), W ** -0.5),
        "sgu_b_out": nrm(ks[20], (DEPTH, D), 0.01),
        "ssm_lam_re": -0.5 + nrm(ks[21], (DEPTH, N_DIR, G, P), 0.01),
        "ssm_lam_im": lam_im0 + nrm(ks[22], (DEPTH, N_DIR, G, P), 0.01),
        "ssm_log_dt": jax.random.uniform(ks[23], (DEPTH, N_DIR, G), f32, math.log(0.001), math.log(0.1)),
        "ssm_b_re": nrm(ks[24], (DEPTH, N_DIR, G, P, I), (2 * I) ** -0.5),
        "ssm_b_im": nrm(ks[25], (DEPTH, N_DIR, G, P, I), (2 * I) ** -0.5),
        "ssm_c_re": nrm(ks[26], (DEPTH, N_DIR, G, I, P), (2 * P) ** -0.5),
        "ssm_c_im": nrm(ks[27], (DEPTH, N_DIR, G, I, P), (2 * P) ** -0.5),
        "ssm_d": nrm(ks[28], (DEPTH, W), 1.0),
        "ssm_w_glu": nrm(ks[29], (DEPTH, W, W), W ** -0.5),
        "ssm_b_glu": nrm(ks[30], (DEPTH, W), 0.01),
        "ssm_w_out": nrm(ks[31], (DEPTH, W, D), W ** -0.5),
        "ssm_b_out": nrm(ks[32], (DEPTH, D), 0.01),
        "w_o": nrm(ks[33], (DEPTH, D, D), D ** -0.5),
        "b_o": nrm(ks[34], (DEPTH, D), 0.01),
        "ffn_w_up": nrm(ks[35], (DEPTH, D, 2 * D_FF), D ** -0.5),
        "ffn_w_down": nrm(ks[36], (DEPTH, D_FF, D), D_FF ** -0.5),
    }


def reference(x, c, ctx, c_ctx, w_ada, b_ada, norm_g, w_in, b_in, conv_w, conv_b, conv_ln_g, conv_ln_b,
              conv_w_out, conv_b_out, sgu_ln_g, sgu_ln_b, sgu_w, sgu_b, sgu_w_out, sgu_b_out,
              ssm_lam_re, ssm_lam_im, ssm_log_dt, ssm_b_re, ssm_b_im, ssm_c_re, ssm_c_im, ssm_d,
              ssm_w_glu, ssm_b_glu, ssm_w_out, ssm_b_out, w_o, b_o, ffn_w_up, ffn_w_down):
    n_b, n_l, _ = x.shape
    ROWS = n_l // GRID_W
    x = x + _grid_pos_embed(ROWS, D_MODEL).astype(x.dtype)[None]
    zero_state = (jnp.zeros((n_b, SSM_GROUPS, SSM_STATE), jnp.float32),
                  jnp.zeros((n_b, SSM_GROUPS, SSM_STATE), jnp.float32))
    s_c = jax.nn.silu(c)
    s_cc = jax.nn.silu(c_ctx)
    for l in range(DEPTH):
        last = l == DEPTH - 1
        lp = {
            'w_in': w_in[l], 'b_in': b_in[l], 'conv_w': conv_w[l], 'conv_b': conv_b[l],
            'conv_ln_g': conv_ln_g[l], 'conv_ln_b': conv_ln_b[l], 'conv_w_out': conv_w_out[l],
            'conv_b_out': conv_b_out[l], 'sgu_ln_g': sgu_ln_g[l], 'sgu_ln_b': sgu_ln_b[l],
            'sgu_w': sgu_w[l], 'sgu_b': sgu_b[l], 'sgu_w_out': sgu_w_out[l], 'sgu_b_out': sgu_b_out[l],
            'lam_re': ssm_lam_re[l], 'lam_im': ssm_lam_im[l], 'log_dt': ssm_log_dt[l],
            'b_re': ssm_b_re[l], 'b_im': ssm_b_im[l], 'c_re': ssm_c_re[l], 'c_im': ssm_c_im[l],
            'ssm_d': ssm_d[l], 'ssm_w_glu': ssm_w_glu[l], 'ssm_b_glu': ssm_b_glu[l],
            'ssm_w_out': ssm_w_out[l], 'ssm_b_out': ssm_b_out[l], 'w_o': w_o[l], 'b_o': b_o[l],
        }
        m_x = jnp.split((s_c @ w_ada[l] + b_ada[l])[:, None, :], 6, axis=-1)
        m_c = jnp.split((s_cc @ w_ada[l] + b_ada[l])[None, None, :], 6, axis=-1)

        hc = _modulate(_rmsnorm(ctx, norm_g[l, 0]), m_c[0], m_c[1])
        if last:
            u_c = (hc @ w_in[l][:, 4 * W_BR:5 * W_BR] + b_in[l][4 * W_BR:5 * W_BR]).astype(jnp.float32)
            hc_f, hc_b = _ssm_scan_pair(u_c, zero_state, zero_state, lp)
        else:
            out_c, hc_f, hc_b = _mix_stream(hc, zero_state, zero_state, lp)
            ctx = ctx + m_c[2] * _rmsnorm(out_c, norm_g[l, 1])
            hc2 = _modulate(_rmsnorm(ctx, norm_g[l, 2]), m_c[3], m_c[4])
            ctx = ctx + m_c[5] * _rmsnorm(_swiglu(hc2, ffn_w_up[l], ffn_w_down[l]), norm_g[l, 3])

        hx = _modulate(_rmsnorm(x, norm_g[l, 0]), m_x[0], m_x[1])
        out_x, _, _ = _mix_stream(hx, _final(hc_f), _final(hc_b), lp)
        x = x + m_x[2] * _rmsnorm(out_x, norm_g[l, 1])
        hx2 = _modulate(_rmsnorm(x, norm_g[l, 2]), m_x[3], m_x[4])
        x = x + m_x[5] * _rmsnorm(_swiglu(hx2, ffn_w_up[l], ffn_w_down[l]), norm_g[l, 3])
    return x
```

```python
import math
import numpy as np
import concourse.bass as bass
import concourse.mybir as mybir
from concourse.bass_utils import run_bass_kernel_spmd

F32 = mybir.dt.float32
BF16 = mybir.dt.bfloat16
AF = mybir.ActivationFunctionType
ALU = mybir.AluOpType

FUSED = True
DEPTH = 4
D = 1024
KT = 8
WBR = 512
T_LAT = 4096
T_CTX = 256
TT = 512
R = 16
DFF = 2816
FT = 22
EPS = 1e-6
TWO_PI = 2.0 * math.pi
NV = 184
V_BIN, V_CONVB, V_CLNG, V_CLNB, V_SLNG, V_SLNB, V_SSMD, V_BGLU = 0, 44, 48, 52, 56, 60, 64, 68
V_CBO, V_SBO, V_SSBO, V_BO, V_NG, V_BADA = 72, 80, 88, 96, 104, 136
C_ID, C_KM, C_GPM, C_TAU, C_KIDX, C_POS, C_OM, C_PH = 0, 128, 256, 258, 275, 535, 599, 603
NCST = 607
NKIDX = 260
WIN_ORDER = [4, 5, 6, 7, 0, 1, 2, 3, 16, 17, 18, 19, 8, 9, 10, 11, 12, 13, 14, 15,
             28, 29, 30, 31, 32, 33, 34, 35, 20, 21, 22, 23, 24, 25, 26, 27, 36, 37, 38, 39, 40, 41, 42, 43]
G_ZA_GATE, G_ZA_A, G_ZS, G_ZB_U, G_ZB_V, G_ZG_B, G_ZG_A, G_ZG_C = 0, 1, 2, 3, 4, 5, 7, 9


class Buf:
    def __init__(self, name):
        self.name = name
        self.w = {}
        self.r = {}
        self.alias = []


class Eng:
    def __init__(self, nc, name, h):
        self.nc, self.name, self.h = nc, name, h
        self.n = 0
        self.gen = 0
        self.sem = nc.alloc_semaphore(name=f"{name}_p0")
        self.semname = f"{name}_p0"
        self.seen = {}
        self.ndma = 0
        self.dsems = None

    def roll(self):
        if self.n >= 30000:
            self.gen += 1
            self.semname = f"{self.name}_p{self.gen}"
            self.sem = self.nc.alloc_semaphore(name=self.semname)
            self.n = 0


class K:
    def __init__(self, nc):
        self.nc = nc
        self.pe = Eng(nc, "pe", nc.tensor)
        self.act = Eng(nc, "act", nc.scalar)
        self.dve = Eng(nc, "dve", nc.vector)
        self.pool = Eng(nc, "pool", nc.gpsimd)
        self.sp = Eng(nc, "sp", nc.sync)
        for q in (self.pool, self.sp):
            q.dsems = [(nc.alloc_semaphore(name=f"{q.name}_d{i}"), f"{q.name}_d{i}") for i in range(8)]
        self.nins = 0
        self.nwait = {}
        self.ncnt = {}

    def _wait(self, eng, toks):
        for (sem, semname, val, _o) in toks:
            if eng.seen.get(semname, 0) >= val:
                continue
            eng.h.wait_ge(sem, val)
            self.nwait[eng.name] = self.nwait.get(eng.name, 0) + 1
            eng.seen[semname] = val

    def _deps(self, eng, r, w, rawself=True):
        toks = []
        for b in r:
            for key, t in b.w.items():
                if key == eng.name and not rawself:
                    continue
                toks.append(t)
        for b0 in w:
            for b in [b0] + b0.alias:
                for key, t in b.w.items():
                    if key != eng.name:
                        toks.append(t)
                for key, t in b.r.items():
                    if key != eng.name:
                        toks.append(t)
        return toks

    def I(self, eng, fn, r=(), w=(), rawself=True, inc=True):
        self._wait(eng, self._deps(eng, r, w, rawself))
        if not getattr(eng, "pending", False):
            eng.roll()
        ins = fn()
        self.ncnt[eng.name] = self.ncnt.get(eng.name, 0) + 1
        if inc:
            eng.n += 1
            ins.then_inc(eng.sem, 1)
            eng.pending = False
            tok = (eng.sem, eng.semname, eng.n, self.nins)
        else:
            eng.pending = True
            tok = (eng.sem, eng.semname, eng.n + 1, self.nins)
        for b in r:
            b.r[eng.name] = tok
        for b0 in w:
            for b in [b0] + b0.alias:
                b.w[eng.name] = tok
        self.nins += 1
        return ins

    def dma(self, q, out, in_, r=(), w=()):
        i = q.ndma
        sem, semname = q.dsems[i % 8]
        prev = 16 * (i // 8)
        if prev > 0:
            self._wait(q, [(sem, semname, prev, 0)])
        self._wait(q, self._deps(q, r, w))
        q.h.dma_start(out=out, in_=in_).then_inc(sem, 16)
        q.ndma += 1
        tok = (sem, semname, prev + 16, self.nins)
        for b in r:
            b.r[semname] = tok
        for b0 in w:
            for b in [b0] + b0.alias:
                b.w[semname] = tok
        self.nins += 1
        return tok

    def final_wait(self, q, bufs):
        toks = []
        for b in bufs:
            toks += list(b.w.values())
        self._wait(q, toks)


def V(t, p0, npart, off, dims):
    fs = 1
    for s in t.shape[1:]:
        fs *= s
    return bass.AP(tensor=t, offset=p0 * fs + off, ap=[[fs, npart]] + [list(d) for d in dims])


def build(layers, first, last_out):
    nc = bass.Bass("TRN2", target_bir_lowering=False)
    k = K(nc)
    I, dma = k.I, k.dma
    pe, act, dve, pool, sp = k.pe, k.act, k.dve, k.pool, k.sp
    L = len(layers)

    def din(name, shape, dt=F32):
        return nc.dram_tensor(name, list(shape), dt, kind="ExternalInput").ap()

    xT = din("xT", [KT, 128, T_LAT])
    ctxT = din("ctxT", [KT, 128, T_CTX])
    sc_in = din("sc_in", [128, KT * 2])
    vecs_d = din("vecs", [L, 128, NV])
    ssmp_d = din("ssmp", [L, 2, 128, 1072])
    swT_d = din("swT", [L, 128, 1024])
    sgb_d = din("sgb", [L, 128, 512])
    cw_d = din("cw", [L, 128, 124])
    wada_d = din("wada", [L, 12, 128, 4096])
    win_d = din("win", [L, 11, 128, 4096])
    wco_d = din("wco", [L, 128, 4096])
    wso_d = din("wso", [L, 128, 4096])
    wss_d = din("wss", [L, 128, 4096])
    wglu_d = din("wglu", [L, 128, 2048])
    wo_d = din("wo", [L, 2, 128, 4096])
    wup_d = din("wup", [L, 11, 128, 4096])
    wdn_d = din("wdn", [L, 8, 128, 2816])
    cst_d = din("cst", [128, NCST])
    xo = nc.dram_tensor("xo", [KT, 128, T_LAT], F32, kind="ExternalOutput").ap()
    co = nc.dram_tensor("co", [KT, 128, T_CTX], F32, kind="ExternalOutput").ap()
    TALL = T_CTX + T_LAT
    APAD = 16
    AW = TALL + 4 * APAD
    a_dr = nc.dram_tensor("a_dr", [128, 4, AW], BF16).ap()
    NGRP = 36
    wsc = nc.dram_tensor("wsc", [2, NGRP, 128, 4096], BF16).ap()
    B_wsc = [[Buf(f"wsc{p_}_{g_}") for g_ in range(NGRP)] for p_ in range(2)]
    GI_WIN, GI_WCO, GI_WSO, GI_WSS, GI_WGLU, GI_WO, GI_WUP, GI_WDN = 0, 11, 12, 13, 14, 15, 17, 28

    def sb(name, shape, dt=F32):
        return nc.alloc_sbuf_tensor("s_" + name, list(shape), dt)

    cst = sb("cst", [128, NCST]); B_cst = Buf("cst")
    identb = sb("identb", [128, 128], BF16)
    onesb = sb("onesb", [128, 128], BF16)
    ones5 = sb("ones5", [128, 128], BF16)
    negpi = sb("negpi", [128, 1])
    kmask = cst[:, C_KM:C_KM + 128]
    emb = sb("emb", [128, 4, 64]); B_emb = Buf("emb")
    scb = sb("scb", [128, KT * 2], BF16); B_scb = Buf("scb")
    vecs = sb("vecs", [128, NV]); B_vecs = Buf("vecs")
    modv = sb("modv", [128, 48, 2]); B_mod = Buf("mod")
    gs = sb("gs", [128, 6, KT, 2]); B_gs = Buf("gs")
    swT = sb("swT", [128, 8, 128], BF16); B_swT = Buf("swT")
    sgb = sb("sgb", [128, 4, 128]); B_sgb = Buf("sgb")
    cw = sb("cw", [128, 4, 31]); B_cw = Buf("cw")
    zs_all = sb("zs_all", [128, 4, TALL], BF16)
    y_all = sb("y_all", [128, 4, TALL], BF16)
    streams = {"ctx": dict(T=T_CTX, off=0, aoff=APAD, N=256, sidx=1),
               "lat": dict(T=T_LAT, off=T_CTX, aoff=T_CTX + 3 * APAD, N=TT, sidx=0)}
    B_zs = {}; B_y = {}; B_ad = {}
    for sn, st in streams.items():
        nt = st["T"] // st["N"]
        B_zs[sn] = [Buf(f"zs{sn}{i}") for i in range(nt)]
        B_y[sn] = [Buf(f"y{sn}{i}") for i in range(nt)]
        B_ad[sn] = [Buf(f"ad{sn}{i}") for i in range(nt)]
    B_apad = Buf("apad")
    rstd = sb("rstd", [128, TT]); B_rstd = Buf("rstd")
    tmpf = [sb(f"tmpf{i}", [128, TT]) for i in range(2)]; B_tmpf = [Buf(f"tmpf{i}") for i in range(2)]
    tctr = [0]

    def ntmp():
        i = tctr[0] % 2
        tctr[0] += 1
        return tmpf[i], B_tmpf[i]
    gateb = [sb(f"gateb{i}", [128, TT], BF16) for i in range(2)]; B_gate = [Buf(f"gate{i}") for i in range(2)]
    gctr = [0]

    def ngate():
        i = gctr[0] % 2
        gctr[0] += 1
        return gateb[i], B_gate[i]

    NPS = 6
    psf = [nc.alloc_psum_tensor(f"psf{i}", [128, 512], F32) for i in range(NPS)]
    B_psf = [Buf(f"psf{i}") for i in range(NPS)]
    psb = [nc.alloc_psum_tensor(f"psb{i}", [128, 1024], BF16) for i in range(2)]
    B_psb = [Buf(f"psb{i}") for i in range(2)]
    pctr = [0, 0]

    def nps():
        i = pctr[0] % NPS
        pctr[0] += 1
        return psf[i], B_psf[i]

    def npsb():
        i = pctr[1] % 2
        pctr[1] += 1
        return psb[i], B_psb[i]

    UB = 121 * 1024
    uni = sb("uni", [128, UB // 2], BF16)
    B_AC_all = []
    B_SSM_all = []
    ucur = [0]

    UFLAT = {}

    def review(name, shape, dt):
        o_el, nb = UFLAT[name]
        n = 1
        for s_ in shape:
            n *= s_
        assert n * (4 if dt == F32 else 2) <= nb, (name, shape)
        v = uni[:, o_el:o_el + nb // 2]
        if dt == F32:
            v = v.bitcast(F32)
        v = v[:, 0:n]
        if len(shape) == 3:
            v = v.rearrange("p (a b c) -> p a b c", a=shape[0], b=shape[1])
        elif len(shape) == 2:
            v = v.rearrange("p (a b) -> p a b", a=shape[0])
        return v

    def ualloc(shape, dt, side, name):
        n = 1
        for s_ in shape:
            n *= s_
        nb = n * (4 if dt == F32 else 2)
        nb = (nb + 63) // 64 * 64
        o_el = ucur[0] // 2
        ucur[0] += nb
        assert ucur[0] <= UB, f"union overflow {name} {ucur[0]}"
        UFLAT[name] = (o_el, nb)
        v = uni[:, o_el:o_el + nb // 2]
        if dt == F32:
            v = v.bitcast(F32)
        v = v[:, 0:n]
        if len(shape) == 2:
            v = v.rearrange("p (a b) -> p a b", a=shape[0])
        elif len(shape) == 3:
            v = v.rearrange("p (a b c) -> p a b c", a=shape[0], b=shape[1])
        elif len(shape) == 4:
            v = v.rearrange("p (a b c d) -> p a b c d", a=shape[0], b=shape[1], c=shape[2])
        b = Buf(name)
        (B_AC_all if side == 0 else B_SSM_all).append(b)
        return v, b

    ucur[0] = 0
    NSLOT = 3
    wslot = []; B_ws = []
    for i in range(NSLOT):
        v, b = ualloc([4096], BF16, 0, f"ws{i}")
        wslot.append(v); B_ws.append(b)
    xt, B_xt = ualloc([KT, TT], F32, 0, "xt")
    hb, B_h = ualloc([KT, TT], BF16, 0, "h")
    big1, B_big1 = ualloc([KT, TT], F32, 0, "big1")
    act_off = ucur[0]
    mergedb, B_mergedb = ualloc([KT, TT], BF16, 0, "mergedb")
    tmp4, B_tmp4 = ualloc([4, TT], F32, 0, "tmp4")
    b4 = []; B_b4 = []
    for i in range(3):
        v, b = ualloc([4, TT], BF16, 0, f"b4{i}")
        b4.append(v); B_b4.append(b)
    pre_end = ucur[0]
    assert pre_end - act_off >= FT * TT * 2
    actb = uni[:, act_off // 2:act_off // 2 + FT * TT].rearrange("p (a b) -> p a b", a=FT)
    B_act = Buf("act"); B_AC_all.append(B_act)
    pre = [B_mergedb, B_tmp4] + B_b4
    B_act.alias = list(pre)
    for b in pre:
        b.alias = [B_act]
    sqb, B_sqb = ualloc([KT, TT], BF16, 0, "sqb")
    diag, B_diag = ualloc([31, 128], BF16, 0, "diag")
    vT, B_vT = ualloc([4, 128], BF16, 0, "vT")
    diag2 = review("sqb", [31, 128], BF16)
    B_diag2 = Buf("diag2"); B_AC_all.append(B_diag2)
    B_diag2.alias = [B_sqb]
    B_sqb.alias = [B_diag2]
    a_ld, B_ald = ualloc([4, TT + 32], BF16, 0, "a_ld")
    side0_end = ucur[0]
    wctr = [0]

    def wload(src_ap, nelem):
        i = wctr[0] % NSLOT
        wctr[0] += 1
        dma(pool, wslot[i][:, 0:nelem], src_ap, w=[B_ws[i]])
        return wslot[i], B_ws[i]

    def wload_sc(par, gi, nelem):
        i = wctr[0] % NSLOT
        wctr[0] += 1
        dma(sp, wslot[i][:, 0:nelem], wsc[par, gi, :, 0:nelem], r=[B_wsc[par][gi]], w=[B_ws[i]])
        return wslot[i], B_ws[i]

    def convert_layer(lj, par):
        def cv(gi, src, n):
            dma(pool, wsc[par, gi, :, 0:n], src, w=[B_wsc[par][gi]])
        for g_ in range(11):
            cv(GI_WIN + g_, win_d[lj, g_, :, :], 4096)
        cv(GI_WCO, wco_d[lj, :, :], 4096)
        cv(GI_WSO, wso_d[lj, :, :], 4096)
        cv(GI_WSS, wss_d[lj, :, :], 4096)
        cv(GI_WGLU, wglu_d[lj, :, :], 2048)
        for g_ in range(2):
            cv(GI_WO + g_, wo_d[lj, g_, :, :], 4096)
        for g_ in range(11):
            cv(GI_WUP + g_, wup_d[lj, g_, :, :], 4096)
        for g_ in range(8):
            cv(GI_WDN + g_, wdn_d[lj, g_, :, :], 2816)

    ucur[0] = 0
    NP = 16
    NCM = T_LAT // R
    NCC = T_CTX // R
    HW = (NCC + 1) + (NCM + 1)
    NCW = max(NCM + 1, 136)
    ssmp, B_ssmp = ualloc([1072], F32, 1, "ssmp")
    t17, B_t17 = ualloc([NP, R + 1], F32, 1, "t17")
    magt, B_magt = ualloc([NP, R + 1], F32, 1, "magt")
    sint, B_sint = ualloc([NP, R + 1], F32, 1, "sint")
    cost, B_cost = ualloc([NP, R + 1], F32, 1, "cost")
    ar, B_apow = ualloc([NP, R + 1], F32, 1, "ar")
    ai, _ = ualloc([NP, R + 1], F32, 1, "ai")
    sm, B_sm = ualloc([12, NP], F32, 1, "sm")
    bbr, B_bb = ualloc([NP, 16], F32, 1, "bbr")
    bbi, _ = ualloc([NP, 16], F32, 1, "bbi")
    bbxr, B_bbx = ualloc([NP, 2, 16], BF16, 1, "bbxr")
    bbxi, _ = ualloc([NP, 2, 16], BF16, 1, "bbxi")
    wt = []; B_w = []
    for i in range(4):
        v, b = ualloc([NP, 16], F32, 1, f"w{i}")
        wt.append(v); B_w.append(b)
    xm = []; B_xm = []
    for i in range(2):
        v, b = ualloc([4, 2, 16], BF16, 1, f"xm{i}")
        xm.append(v); B_xm.append(b)
    xmall = []; B_xmall = []
    for i in range(2):
        v, b = ualloc([R + 1, 4, 2, 16], BF16, 1, f"xmall{i}")
        xmall.append(v); B_xmall.append(b)
    Pk, B_P = ualloc([R, 2, 128], BF16, 1, "Pk")
    Qk, B_Q = ualloc([4, 2, R, 32], BF16, 1, "Qk")
    Kk, B_K = ualloc([R, 128], BF16, 1, "Kk")
    SH, B_SH = ualloc([4, 2, NCW], F32, 1, "SH")
    TK, B_TK = ualloc([4, 2, NCW], F32, 1, "TK")
    Marr, B_M = ualloc([4, 2, NCW], F32, 1, "Marr")
    Garr, B_G = ualloc([4, 2, NCW], F32, 1, "Garr")
    cosk, B_ck = ualloc([4, NCM + 1], F32, 1, "cosk")
    sink, _ = ualloc([4, NCM + 1], F32, 1, "sink")
    angk, B_angk = ualloc([4, NCM + 1], F32, 1, "angk")
    Hk, B_H = ualloc([4, 2, HW], BF16, 1, "Hk")
    hfin, B_hfin = ualloc([4, 2], F32, 1, "hfin")
    YI, B_YI = ualloc([R, 4, 32], BF16, 1, "YI")
    NRED = max(4 * (NCM + 1), NP * (R + 1))
    redf, B_red = ualloc([NRED], F32, 1, "redf")
    redi_f, _ = ualloc([NRED], F32, 1, "redi")
    redi = redi_f.bitcast(mybir.dt.int32)
    redf3 = redf[:, 0:4 * (NCM + 1)].rearrange("p (a b) -> p a b", a=4)
    redi3 = redi[:, 0:4 * (NCM + 1)].rearrange("p (a b) -> p a b", a=4)
    ystage, B_yst = ualloc([TT], F32, 1, "ystage")
    side1_end = ucur[0]
    print("union side0", side0_end, "side1", side1_end, "sbuf remaining", nc.sbuf_bytes_remaining)
    SHW = NCM + 1

    def phase_barrier(new_side):
        old = B_SSM_all if new_side == 0 else B_AC_all
        new = B_AC_all if new_side == 0 else B_SSM_all
        merged = {}
        for ob in old:
            for key, t in list(ob.w.items()) + list(ob.r.items()):
                kk = "bar_" + key
                if kk not in merged or merged[kk][3] < t[3]:
                    merged[kk] = t
        for b in new:
            for kk, t in merged.items():
                if kk not in b.r or b.r[kk][3] < t[3]:
                    b.r[kk] = t

    dma(sp, cst[:, :], cst_d[:, :], w=[B_cst])
    I(dve, lambda: nc.vector.tensor_copy(out=identb[:, :], in_=cst[:, C_ID:C_ID + 128]), r=[B_cst], w=[B_cst])
    I(dve, lambda: nc.vector.memset(onesb[:, :], 1.0 / 1024), w=[B_cst])
    I(dve, lambda: nc.vector.memset(ones5[:, :], 1.0 / 512), w=[B_cst])
    I(dve, lambda: nc.vector.memset(negpi[:, :], -math.pi), w=[B_cst])
    epsb = sb("epsb", [128, 1])
    I(dve, lambda: nc.vector.memset(epsb[:, :], EPS), w=[B_cst])
    I(dve, lambda: nc.vector.memset(a_ld[:, :, 0:APAD], 0.0), w=[B_ald])
    for off in (0, APAD + T_CTX, T_CTX + 2 * APAD, T_CTX + 3 * APAD + T_LAT):
        dma(sp, a_dr[:, :, off:off + APAD], a_ld[:, :, 0:APAD], r=[B_ald], w=[B_apad])
    if first:
        I(dve, lambda: nc.vector.tensor_tensor(out=emb[:, :, :], in0=cst[:, C_OM:C_OM + 4].unsqueeze(2).to_broadcast([128, 4, 64]),
                                               in1=cst[:, C_POS:C_POS + 64].unsqueeze(1).to_broadcast([128, 4, 64]), op=ALU.mult),
          r=[B_cst], w=[B_emb])
        I(dve, lambda: nc.vector.tensor_tensor(out=emb[:, :, :], in0=emb[:, :, :],
                                               in1=cst[:, C_PH:C_PH + 4].unsqueeze(2).to_broadcast([128, 4, 64]), op=ALU.add),
          r=[B_cst, B_emb], w=[B_emb])
    sc32 = sb("sc32", [128, KT * 2])
    dma(sp, sc32[:, :], sc_in[:, :], w=[B_scb])
    sct = sb("sct", [128, KT * 2])
    I(act, lambda: nc.scalar.activation(out=sct[:, :], in_=sc32[:, :], func=AF.Sigmoid), r=[B_scb], w=[B_scb])
    I(dve, lambda: nc.vector.tensor_tensor(out=scb[:, :], in0=sct[:, :], in1=sc32[:, :], op=ALU.mult), r=[B_scb], w=[B_scb])
    scb3 = scb[:, :].rearrange("p (k s) -> p k s", s=2)

    def mm(out, lhsT, rhs, start, stop, r, w, tp=None):
        if tp is None:
            return I(pe, lambda: nc.tensor.matmul(out, lhsT=lhsT, rhs=rhs, start=start, stop=stop), r=r, w=w, rawself=False,
                     inc=stop)
        return I(pe, lambda: nc.tensor.matmul(out, lhsT=lhsT, rhs=rhs, start=start, stop=stop, tile_position=tp),
                 r=r, w=w, rawself=False, inc=stop)

    def vv(out, in0, in1, op, r, w):
        I(dve, lambda: nc.vector.tensor_tensor(out=out, in0=in0, in1=in1, op=op), r=r, w=w)


    C1_2PI = 6.28125
    C2_2PI = TWO_PI - 6.28125
    PI_LO = 3.1415925

    def reduce_angle(ang, ti, tf, rB, Bang, Bt):
        I(dve, lambda: nc.vector.tensor_scalar(out=tf, in0=ang, scalar1=1.0 / TWO_PI, scalar2=None, op0=ALU.mult),
          r=rB + [Bang], w=[Bt])
        I(dve, lambda: nc.vector.tensor_copy(out=ti, in_=tf), r=[Bt], w=[Bt])
        I(dve, lambda: nc.vector.tensor_copy(out=tf, in_=ti), r=[Bt], w=[Bt])
        I(dve, lambda: nc.vector.scalar_tensor_tensor(out=ang, in0=tf, scalar=-C1_2PI, in1=ang, op0=ALU.mult, op1=ALU.add),
          r=[Bt, Bang], w=[Bang])
        I(dve, lambda: nc.vector.scalar_tensor_tensor(out=ang, in0=tf, scalar=-C2_2PI, in1=ang, op0=ALU.mult, op1=ALU.add),
          r=[Bt, Bang], w=[Bang])
        I(dve, lambda: nc.vector.tensor_scalar(out=ang, in0=ang, scalar1=-PI_LO, scalar2=PI_LO, op0=ALU.max, op1=ALU.min),
          r=[Bang], w=[Bang])

    def sincos(ang, sin_o, cos_o, Bang, Bs, Bc):
        I(act, lambda: nc.scalar.activation(out=sin_o, in_=ang, func=AF.Sin), r=[Bang], w=[Bs])
        I(act, lambda: nc.scalar.activation(out=cos_o, in_=ang, func=AF.Sin, scale=0.5), r=[Bang], w=[Bc])
        I(dve, lambda: nc.vector.tensor_tensor(out=cos_o, in0=cos_o, in1=cos_o, op=ALU.mult), r=[Bc], w=[Bc])
        I(dve, lambda: nc.vector.tensor_scalar(out=cos_o, in0=cos_o, scalar1=-2.0, scalar2=1.0, op0=ALU.mult, op1=ALU.add),
          r=[Bc], w=[Bc])

    def rms_stats(src, Bsrc, N):
        I(act, lambda: nc.scalar.activation(out=sqb[:, :, 0:N], in_=src[:, :, 0:N], func=AF.Square), r=[Bsrc], w=[B_sqb])
        p, Bp = nps()
        for kt in range(KT):
            mm(p[:, 0:N], onesb[:, :], sqb[:, kt, 0:N], kt == 0, kt == KT - 1, r=[B_sqb, B_cst], w=[Bp])
        I(act, lambda: nc.scalar.activation(out=rstd[:, 0:N], in_=p[:, 0:N], func=AF.Sqrt, bias=epsb[:, 0:1], scale=1.0),
          r=[Bp, B_cst], w=[B_rstd])
        I(dve, lambda: nc.vector.reciprocal(out=rstd[:, 0:N], in_=rstd[:, 0:N]), r=[B_rstd], w=[B_rstd])

    def rms_mod(N, gi, si, sidx):
        rms_stats(xt, B_xt, N)
        for kt in range(KT):
            t, Bt = ntmp()
            I(dve, lambda: nc.vector.scalar_tensor_tensor(out=t[:, 0:N], in0=xt[:, kt, 0:N], scalar=gs[:, gi, kt, sidx:sidx + 1],
                                                          in1=rstd[:, 0:N], op0=ALU.mult, op1=ALU.mult),
              r=[B_xt, B_gs, B_rstd], w=[Bt])
            I(act, lambda: nc.scalar.activation(out=hb[:, kt, 0:N], in_=t[:, 0:N], func=AF.Identity,
                                                bias=gs[:, si, kt, sidx:sidx + 1], scale=1.0), r=[Bt, B_gs], w=[B_h])

    def resid_update(src, Bsrc, N, gi, sidx):
        rms_stats(src, Bsrc, N)
        for kt in range(KT):
            t, Bt = ntmp()
            I(dve, lambda: nc.vector.scalar_tensor_tensor(out=t[:, 0:N], in0=src[:, kt, 0:N], scalar=gs[:, gi, kt, sidx:sidx + 1],
                                                          in1=rstd[:, 0:N], op0=ALU.mult, op1=ALU.mult),
              r=[Bsrc, B_gs, B_rstd], w=[Bt])
            I(dve, lambda: nc.vector.tensor_tensor(out=xt[:, kt, 0:N], in0=xt[:, kt, 0:N], in1=t[:, 0:N], op=ALU.add),
              r=[B_xt, Bt], w=[B_xt])

    def proj(ws, Bws, j, nk, rhs_fn, rB, N):
        p, Bp = nps()
        for kt in range(nk):
            o0 = (j * nk + kt) * 128
            mm(p[:, 0:N], ws[:, o0:o0 + 128], rhs_fn(kt), kt == 0, kt == nk - 1, r=[Bws] + rB, w=[Bp])
        return p, Bp

    if first:
        embf = big1[:, 0, 0:256].rearrange("p (a b) -> p a b", a=4)
        embi = big1[:, 1, 0:256].bitcast(mybir.dt.int32).rearrange("p (a b) -> p a b", a=4)
        reduce_angle(emb[:, :, :], embi, embf, [], B_emb, B_big1)
        I(act, lambda: nc.scalar.activation(out=emb[:, :, :], in_=emb[:, :, :], func=AF.Sin), r=[B_emb], w=[B_emb])
    B_xd = {"ctx": [Buf("xdc0")], "lat": [Buf(f"xdl{i}") for i in range(T_LAT // TT)]}

    def load_x(sn, ti, src):
        N = streams[sn]["N"]
        t0 = ti * N
        dma(sp, xt[:, :, 0:N], src[:, :, t0:t0 + N].rearrange("k p t -> p k t"), r=[B_xd[sn][ti]], w=[B_xt])

    def store_x(sn, ti, dst):
        N = streams[sn]["N"]
        t0 = ti * N
        dma(sp, dst[:, :, t0:t0 + N].rearrange("k p t -> p k t"), xt[:, :, 0:N], r=[B_xt], w=[B_xd[sn][ti]])

    for li, l in enumerate(layers):
        from_input = (li == 0)
        add_pe = first and li == 0
        lastl = (l == DEPTH - 1)
        par = li % 2
        if li == 0:
            convert_layer(0, 0)
        if li > 0:
            phase_barrier(0)
        dma(sp, vecs[:, :], vecs_d[li, :, :], w=[B_vecs])
        dma(pool, swT[:, :, :].rearrange("p a b -> p (a b)"), swT_d[li, :, :], w=[B_swT])
        dma(sp, sgb[:, :, :].rearrange("p a b -> p (a b)"), sgb_d[li, :, :], w=[B_sgb])
        dma(sp, cw[:, :, :].rearrange("p a b -> p (a b)"), cw_d[li, :, :], w=[B_cw])
        pm, Bpm = nps()
        for g in range(12):
            ws, Bws = wload(wada_d[li, g, :, :], 4096)
            for j in range(4):
                mt = g * 4 + j
                for kt in range(KT):
                    o0 = (j * KT + kt) * 128
                    mm(pm[:, 2 * mt:2 * mt + 2], ws[:, o0:o0 + 128], scb3[:, kt, :], kt == 0, kt == KT - 1,
                       r=[Bws, B_scb], w=[Bpm])
        I(dve, lambda: nc.vector.tensor_tensor(out=modv[:, :, :], in0=pm[:, 0:96].rearrange("p (m s) -> p m s", s=2),
                                               in1=vecs[:, V_BADA:V_BADA + 48].unsqueeze(2).to_broadcast([128, 48, 2]), op=ALU.add),
          r=[Bpm, B_vecs], w=[B_mod])

        def ng(i):
            return vecs[:, V_NG + 8 * i:V_NG + 8 * i + 8].unsqueeze(2).to_broadcast([128, 8, 2])
        I(dve, lambda: nc.vector.scalar_tensor_tensor(out=gs[:, 0, :, :], in0=modv[:, 8:16, :], scalar=1.0, in1=ng(0),
                                                      op0=ALU.add, op1=ALU.mult), r=[B_mod, B_vecs], w=[B_gs])
        I(dve, lambda: nc.vector.tensor_copy(out=gs[:, 1, :, :], in_=modv[:, 0:8, :]), r=[B_mod], w=[B_gs])
        I(dve, lambda: nc.vector.tensor_tensor(out=gs[:, 2, :, :], in0=modv[:, 16:24, :], in1=ng(1), op=ALU.mult),
          r=[B_mod, B_vecs], w=[B_gs])
        I(dve, lambda: nc.vector.scalar_tensor_tensor(out=gs[:, 3, :, :], in0=modv[:, 32:40, :], scalar=1.0, in1=ng(2),
                                                      op0=ALU.add, op1=ALU.mult), r=[B_mod, B_vecs], w=[B_gs])
        I(dve, lambda: nc.vector.tensor_copy(out=gs[:, 4, :, :], in_=modv[:, 24:32, :]), r=[B_mod], w=[B_gs])
        I(dve, lambda: nc.vector.tensor_tensor(out=gs[:, 5, :, :], in0=modv[:, 40:48, :], in1=ng(3), op=ALU.mult),
          r=[B_mod, B_vecs], w=[B_gs])

        for sn in ("ctx", "lat"):
            st = streams[sn]
            N, sidx = st["N"], st["sidx"]
            for ti in range(st["T"] // N):
                t0 = ti * N
                if from_input:
                    load_x(sn, ti, ctxT if sn == "ctx" else xT)
                    if add_pe and sn == "lat":
                        r0 = t0 // 64
                        nr = N // 64
                        for kt in range(4):
                            I(dve, lambda: nc.vector.tensor_tensor(
                                out=xt[:, kt, 0:N].rearrange("p (r c) -> p r c", c=64),
                                in0=xt[:, kt, 0:N].rearrange("p (r c) -> p r c", c=64),
                                in1=emb[:, kt, r0:r0 + nr].unsqueeze(2).to_broadcast([128, nr, 64]), op=ALU.add),
                              r=[B_xt, B_emb], w=[B_xt])
                            I(dve, lambda: nc.vector.tensor_tensor(
                                out=xt[:, 4 + kt, 0:N].rearrange("p (r c) -> p r c", c=64),
                                in0=xt[:, 4 + kt, 0:N].rearrange("p (r c) -> p r c", c=64),
                                in1=emb[:, kt, 0:64].unsqueeze(1).to_broadcast([128, nr, 64]), op=ALU.add),
                              r=[B_xt, B_emb], w=[B_xt])
                    store_x(sn, ti, co if sn == "ctx" else xo)
                else:
                    load_x(sn, ti, co if sn == "ctx" else xo)
                rms_mod(N, 0, 1, sidx)
                only_ssm = lastl and sn == "ctx"
                if not only_ssm:
                    wsg, Bwsg = wload_sc(par, GI_WIN + G_ZA_GATE, 4096)
                    wsa, Bwsa = wload_sc(par, GI_WIN + G_ZA_A, 4096)
                    ast, B_ast = b4[0], B_b4[0]
                    for j in range(4):
                        pg, Bpg = proj(wsg, Bwsg, j, KT, lambda kt: hb[:, kt, 0:N], [B_h], N)
                        gt, Bgt = ngate()
                        I(act, lambda: nc.scalar.activation(out=gt[:, 0:N], in_=pg[:, 0:N], func=AF.Sigmoid,
                                                            bias=vecs[:, V_BIN + 4 + j:V_BIN + 5 + j], scale=1.0),
                          r=[Bpg, B_vecs], w=[Bgt])
                        pa, Bpa = proj(wsa, Bwsa, j, KT, lambda kt: hb[:, kt, 0:N], [B_h], N)
                        I(dve, lambda: nc.vector.scalar_tensor_tensor(out=ast[:, j, 0:N], in0=pa[:, 0:N],
                                                                      scalar=vecs[:, V_BIN + j:V_BIN + j + 1], in1=gt[:, 0:N],
                                                                      op0=ALU.add, op1=ALU.mult),
                          r=[Bpa, B_vecs, Bgt], w=[B_ast])
                    a0 = st["aoff"] + t0
                    dma(sp, a_dr[:, :, a0:a0 + N], ast[:, :, 0:N], r=[B_ast], w=[B_ad[sn][ti]])
                wsz, Bwsz = wload_sc(par, GI_WIN + G_ZS, 4096)
                for j in range(4):
                    pz, Bpz = proj(wsz, Bwsz, j, KT, lambda kt: hb[:, kt, 0:N], [B_h], N)
                    z0 = st["off"] + t0
                    I(act, lambda: nc.scalar.activation(out=zs_all[:, j, z0:z0 + N], in_=pz[:, 0:N], func=AF.Identity,
                                                        bias=vecs[:, V_BIN + 16 + j:V_BIN + 17 + j], scale=1.0),
                      r=[Bpz, B_vecs], w=[B_zs[sn][ti]])

        phase_barrier(1)
        if li + 1 < len(layers):
            convert_layer(li + 1, (li + 1) % 2)
        for d in range(2):
            fwd = (d == 0)
            dma(sp, ssmp[:, :], ssmp_d[li, d, :, :], w=[B_ssmp])
            lam_re, lam_im, logdt = ssmp[:, 0:16], ssmp[:, 16:32], ssmp[:, 32:48]
            b_re = ssmp[:, 48:304].rearrange("p (a b) -> p a b", b=16)
            b_im = ssmp[:, 304:560].rearrange("p (a b) -> p a b", b=16)
            c_re = ssmp[:, 560:816].rearrange("p (a b) -> p a b", b=16)
            c_im = ssmp[:, 816:1072].rearrange("p (a b) -> p a b", b=16)
            DT, LR, ANG, DEN, FR, FI, T1, T2, PH, RHO = [sm[:, i, :] for i in range(10)]
            tau = cst[:, C_TAU:C_TAU + R + 1]
            I(act, lambda: nc.scalar.activation(out=DT, in_=logdt, func=AF.Exp), r=[B_ssmp], w=[B_sm])
            vv(LR, lam_re, DT, ALU.mult, [B_ssmp, B_sm], [B_sm])
            vv(ANG, lam_im, DT, ALU.mult, [B_ssmp, B_sm], [B_sm])

            def b17(x):
                return x.unsqueeze(2).to_broadcast([128, NP, R + 1])
            tau3 = tau.unsqueeze(1).to_broadcast([128, NP, R + 1])
            vv(t17[:, :, :], b17(LR), tau3, ALU.mult, [B_sm, B_cst], [B_t17])
            I(act, lambda: nc.scalar.activation(out=magt[:, :, :], in_=t17[:, :, :], func=AF.Exp), r=[B_t17], w=[B_magt])
            vv(t17[:, :, :], b17(ANG), tau3, ALU.mult, [B_sm, B_cst, B_magt], [B_t17])
            reduce_angle(t17[:, :, :], redi[:, 0:NP * (R + 1)].rearrange("p (a b) -> p a b", b=R + 1),
                         redf[:, 0:NP * (R + 1)].rearrange("p (a b) -> p a b", b=R + 1), [], B_t17, B_red)
            sincos(t17[:, :, :], sint[:, :, :], cost[:, :, :], B_t17, B_sint, B_cost)
            vv(ar[:, :, :], magt[:, :, :], cost[:, :, :], ALU.mult, [B_magt, B_cost], [B_apow])
            vv(ai[:, :, :], magt[:, :, :], sint[:, :, :], ALU.mult, [B_magt, B_sint], [B_apow])
            a_re, a_im = ar[:, :, 1], ai[:, :, 1]
            vv(DEN, lam_re, lam_re, ALU.mult, [B_ssmp], [B_sm])
            vv(T1, lam_im, lam_im, ALU.mult, [B_ssmp], [B_sm])
            vv(DEN, DEN, T1, ALU.add, [B_sm], [B_sm])
            I(dve, lambda: nc.vector.reciprocal(out=DEN, in_=DEN), r=[B_sm], w=[B_sm])
            I(dve, lambda: nc.vector.tensor_scalar(out=T1, in0=a_re, scalar1=-1.0, scalar2=None, op0=ALU.add),
              r=[B_apow, B_sm], w=[B_sm])
            vv(FR, T1, lam_re, ALU.mult, [B_sm, B_ssmp], [B_sm])
            vv(T2, a_im, lam_im, ALU.mult, [B_apow, B_ssmp, B_sm], [B_sm])
            vv(FR, FR, T2, ALU.add, [B_sm], [B_sm])
            vv(FR, FR, DEN, ALU.mult, [B_sm], [B_sm])
            vv(FI, a_im, lam_re, ALU.mult, [B_apow, B_ssmp, B_sm], [B_sm])
            vv(T2, T1, lam_im, ALU.mult, [B_sm, B_ssmp], [B_sm])
            vv(FI, FI, T2, ALU.subtract, [B_sm], [B_sm])
            vv(FI, FI, DEN, ALU.mult, [B_sm], [B_sm])
            I(dve, lambda: nc.vector.tensor_scalar(out=PH, in0=ANG, scalar1=float(R), scalar2=None, op0=ALU.mult),
              r=[B_sm], w=[B_sm])
            reduce_angle(PH, redi[:, 0:NP], redf[:, 0:NP], [], B_sm, B_red)
            I(dve, lambda: nc.vector.tensor_copy(out=RHO, in_=magt[:, :, R]), r=[B_magt, B_sm], w=[B_sm])

            def bq(x, nq):
                return x.unsqueeze(2).to_broadcast([128, nq, 16])

            def cmul(outr, Boutr, outi, Bouti, xr, xi, yr, yi, rB, nq):
                ta, tb = wt[0][:, 0:nq, :], wt[1][:, 0:nq, :]
                Ba, Bb = B_w[0], B_w[1]
                vv(ta, xr, yr, ALU.mult, rB, [Ba])
                vv(tb, xi, yi, ALU.mult, rB, [Bb])
                vv(outr, ta, tb, ALU.subtract, [Ba, Bb], [Boutr])
                vv(ta, xr, yi, ALU.mult, rB + [Boutr], [Ba])
                vv(tb, xi, yr, ALU.mult, rB + [Boutr], [Bb])
                vv(outi, ta, tb, ALU.add, [Ba, Bb], [Bouti])
            cmul(bbr[:, :, :], B_bb, bbi[:, :, :], B_bb, bq(FR, NP), bq(FI, NP), b_re, b_im, [B_sm, B_ssmp], NP)
            gpm = cst[:, C_GPM:C_GPM + 2]

            def masked(out, src, neg, rB, Bw, nq):
                for g_ in range(2):
                    I(dve, lambda: nc.vector.tensor_scalar(out=out[:, :, g_, :], in0=src, scalar1=gpm[:, g_:g_ + 1],
                                                           scalar2=(-1.0 if neg else 1.0), op0=ALU.mult, op1=ALU.mult),
                      r=rB + [B_cst], w=[Bw])
            masked(bbxr[:, :, :, :], bbr[:, :, :], False, [B_bb], B_bbx, NP)
            masked(bbxi[:, :, :, :], bbi[:, :, :], False, [B_bb], B_bbx, NP)
            w3, w4 = wt[2][:, 0:4, :], wt[3][:, 0:4, :]

            for kt in range(4):
                sl = slice(4 * kt, 4 * kt + 4)
                ta4 = review("SH", [R + 1, 4, 16], F32); tb4 = review("TK", [R + 1, 4, 16], F32)
                Wr4 = review("Marr", [R + 1, 4, 16], F32); Wi4 = review("Garr", [R + 1, 4, 16], F32)

                def AP4(base, q0, e0, dims):
                    ps_ = base.ap[0][0]
                    Bn = base.ap[1][0]
                    return bass.AP(tensor=base.tensor, offset=base.offset + q0 * Bn + e0, ap=[[ps_, 128]] + [list(d_) for d_ in dims])

                def cm4(nt, xr, xi, yr, yi, rB):
                    vv(ta4[:, 0:nt], xr, yr, ALU.mult, rB, [B_SH])
                    vv(tb4[:, 0:nt], xi, yi, ALU.mult, rB, [B_TK])
                    vv(Wr4[:, 0:nt], ta4[:, 0:nt], tb4[:, 0:nt], ALU.subtract, [B_SH, B_TK], [B_M])
                    vv(ta4[:, 0:nt], xr, yi, ALU.mult, rB + [B_M], [B_SH])
                    vv(tb4[:, 0:nt], xi, yr, ALU.mult, rB + [B_M], [B_TK])
                    vv(Wi4[:, 0:nt], ta4[:, 0:nt], tb4[:, 0:nt], ALU.add, [B_SH, B_TK], [B_G])

                def masked4(nt, t_lo, src, Bsrc, dst, Bdst, neg):
                    for g_ in range(2):
                        I(dve, lambda: nc.vector.tensor_scalar(out=dst[:, t_lo:t_lo + nt, :, g_, :], in0=src[:, t_lo:t_lo + nt],
                                                               scalar1=gpm[:, g_:g_ + 1], scalar2=(-1.0 if neg else 1.0),
                                                               op0=ALU.mult, op1=ALU.mult), r=[Bsrc, B_cst], w=[Bdst])
                e0 = (R - 1) if fwd else 0
                dtau = -1 if fwd else 1
                cm4(R, AP4(ar, 4 * kt, e0, [[dtau, R], [R + 1, 4], [0, 16]]), AP4(ai, 4 * kt, e0, [[dtau, R], [R + 1, 4], [0, 16]]),
                    AP4(bbr, 4 * kt, 0, [[0, R], [16, 4], [1, 16]]), AP4(bbi, 4 * kt, 0, [[0, R], [16, 4], [1, 16]]), [B_apow, B_bb])
                masked4(R, 0, Wr4, B_M, xmall[0], B_xmall[0], False)
                masked4(R, 0, Wi4, B_G, xmall[1], B_xmall[1], False)
                for s0 in range(0, R, 4):
                    pb_, Bpb = npsb()
                    for sl_ in range(4):
                        for reim in range(2):
                            I(pe, lambda: nc.tensor.transpose(out=pb_[:, (sl_ * 2 + reim) * 128:(sl_ * 2 + reim + 1) * 128],
                                                              in_=xmall[reim][:, s0 + sl_, :, :, :].rearrange("p a b c -> p (a b c)"),
                                                              identity=identb[:, :]), r=[B_xmall[reim], B_cst], w=[Bpb], rawself=False)
                    I(act, lambda: nc.scalar.copy(out=Pk[:, s0:s0 + 4, :, :].rearrange("p s k m -> p (s k m)"), in_=pb_[:, 0:1024]),
                      r=[Bpb], w=[B_P])
                for sn in ("ctx", "lat"):
                    st = streams[sn]
                    Nc = st["T"] // R
                    zoff = st["off"]
                    hbase = 0 if sn == "ctx" else (NCC + 1)
                    I(dve, lambda: nc.vector.tensor_tensor(out=angk[:, :, 0:Nc + 1],
                                                           in0=PH[:, sl].unsqueeze(2).to_broadcast([128, 4, Nc + 1]),
                                                           in1=cst[:, C_KIDX:C_KIDX + Nc + 1].unsqueeze(1).to_broadcast([128, 4, Nc + 1]),
                                                           op=ALU.mult), r=[B_sm, B_cst], w=[B_angk])
                    reduce_angle(angk[:, :, 0:Nc + 1], redi3[:, :, 0:Nc + 1], redf3[:, :, 0:Nc + 1], [], B_angk, B_red)
                    sincos(angk[:, :, 0:Nc + 1], sink[:, :, 0:Nc + 1], cosk[:, :, 0:Nc + 1], B_angk, B_ck, B_ck)
                    for q in range(4):
                        for reim in range(2):
                            p, Bp = nps()
                            for s_ in range(R):
                                rhs = V(zs_all, 32 * q, 32, kt * TALL + zoff + s_, [[R, Nc]])
                                mm(p[:, 0:Nc], Pk[32 * q:32 * q + 32, s_, reim, :], rhs, s_ == 0, s_ == R - 1,
                                   r=[B_P] + B_zs[sn], w=[Bp], tp=(32 * q, 0))
                            if fwd:
                                dst = SH[:, q, reim, 0:Nc]
                            else:
                                dst = SH[:, q, reim, 0:Nc][:, ::-1]
                            I(act, lambda: nc.scalar.copy(out=dst, in_=p[:, 0:Nc]), r=[Bp], w=[B_SH])
                    ck = cosk[:, :, 1:Nc + 1]
                    sk = sink[:, :, 1:Nc + 1]
                    Sr, Si = SH[:, :, 0, 0:Nc], SH[:, :, 1, 0:Nc]
                    t1r, t1i = TK[:, :, 0, 0:Nc], TK[:, :, 1, 0:Nc]
                    vv(t1r, Sr, ck, ALU.mult, [B_SH, B_ck], [B_TK])
                    vv(t1i, Si, sk, ALU.mult, [B_SH, B_ck], [B_TK])
                    vv(Marr[:, :, 0, 0:Nc], t1r, t1i, ALU.add, [B_TK], [B_M])
                    vv(t1r, Si, ck, ALU.mult, [B_SH, B_ck, B_M], [B_TK])
                    vv(t1i, Sr, sk, ALU.mult, [B_SH, B_ck, B_M], [B_TK])
                    vv(Marr[:, :, 1, 0:Nc], t1r, t1i, ALU.subtract, [B_TK], [B_M])
                    if sn == "ctx":
                        I(dve, lambda: nc.vector.memset(Garr[:, :, :, 0], 0.0), w=[B_G])
                    else:
                        I(dve, lambda: nc.vector.tensor_copy(out=Garr[:, :, :, 0], in_=hfin[:, :, :]), r=[B_hfin], w=[B_G])
                    for q in range(4):
                        for reim in range(2):
                            I(dve, lambda: nc.vector.tensor_tensor_scan(
                                out=Garr[:, q, reim, 1:Nc + 1], data0=RHO[:, 4 * kt + q:4 * kt + q + 1].to_broadcast([128, Nc]),
                                data1=Marr[:, q, reim, 0:Nc], initial=Garr[:, q, reim, 0:1], op0=ALU.mult, op1=ALU.add),
                              r=[B_M, B_sm, B_G], w=[B_G])
                    cj, sj = cosk[:, :, 0:Nc + 1], sink[:, :, 0:Nc + 1]
                    Gr, Gi = Garr[:, :, 0, 0:Nc + 1], Garr[:, :, 1, 0:Nc + 1]
                    u1, u2 = TK[:, :, 0, 0:Nc + 1], TK[:, :, 1, 0:Nc + 1]
                    vv(u1, Gr, cj, ALU.mult, [B_G, B_ck], [B_TK])
                    vv(u2, Gi, sj, ALU.mult, [B_G, B_ck], [B_TK])
                    vv(SH[:, :, 0, 0:Nc + 1], u1, u2, ALU.subtract, [B_TK], [B_SH])
                    vv(u1, Gi, cj, ALU.mult, [B_G, B_ck, B_SH], [B_TK])
                    vv(u2, Gr, sj, ALU.mult, [B_G, B_ck, B_SH], [B_TK])
                    vv(SH[:, :, 1, 0:Nc + 1], u1, u2, ALU.add, [B_TK], [B_SH])
                    I(act, lambda: nc.scalar.copy(out=Hk[:, :, :, hbase:hbase + Nc + 1], in_=SH[:, :, :, 0:Nc + 1]),
                      r=[B_SH], w=[B_H])
                    if sn == "ctx":
                        I(dve, lambda: nc.vector.tensor_copy(out=hfin[:, :, :], in_=SH[:, :, :, Nc]), r=[B_SH], w=[B_hfin])
                c_re3 = c_re
                c_im3 = c_im
                cm4(R + 1, AP4(c_re3, 4 * kt, 0, [[0, R + 1], [16, 4], [1, 16]]), AP4(c_im3, 4 * kt, 0, [[0, R + 1], [16, 4], [1, 16]]),
                    AP4(ar, 4 * kt, 0, [[1, R + 1], [R + 1, 4], [0, 16]]), AP4(ai, 4 * kt, 0, [[1, R + 1], [R + 1, 4], [0, 16]]),
                    [B_apow, B_ssmp])
                masked4(R + 1, 0, Wr4, B_M, xmall[0], B_xmall[0], False)
                masked4(R + 1, 0, Wi4, B_G, xmall[1], B_xmall[1], True)
                for t0_ in range(0, R, 4):
                    p, Bp = nps()
                    mm(p[:, 0:512], bbxr[:, sl, :, :].rearrange("p a b c -> p (a b c)"),
                       xmall[0][:, t0_:t0_ + 4, :, :, :].rearrange("p t a b c -> p (t a b c)"), True, False, r=[B_bbx, B_xmall[0]], w=[Bp])
                    mm(p[:, 0:512], bbxi[:, sl, :, :].rearrange("p a b c -> p (a b c)"),
                       xmall[1][:, t0_:t0_ + 4, :, :, :].rearrange("p t a b c -> p (t a b c)"), False, True, r=[B_bbx, B_xmall[1]], w=[Bp])
                    I(dve, lambda: nc.vector.tensor_tensor(out=Kk[:, t0_:t0_ + 4, :], in0=p[:, 0:512].rearrange("p (t m) -> p t m", t=4),
                                                           in1=kmask.unsqueeze(1).to_broadcast([128, 4, 128]), op=ALU.mult),
                      r=[Bp, B_cst], w=[B_K])
                psq = Qk.ap[0][0]
                for reim in range(2):
                    s_start = 0 if fwd else (R - 1)
                    s_step = 32 if fwd else -32
                    qout = bass.AP(tensor=Qk.tensor, offset=Qk.offset + reim * R * 32 + s_start * 32,
                                   ap=[[psq, 128], [s_step, R], [2 * R * 32, 4], [1, 32]])
                    I(dve, lambda: nc.vector.tensor_copy(out=qout, in_=xmall[reim][:, 1:R + 1, :, :, :].rearrange("p t q g j -> p t q (g j)")),
                      r=[B_xmall[reim]], w=[B_Q])
                for sn in ("ctx", "lat"):
                    if sn == "ctx" and lastl:
                        continue
                    st = streams[sn]
                    Nc = st["T"] // R
                    zoff = st["off"]
                    hbase = 0 if sn == "ctx" else (NCC + 1)
                    N = st["N"]
                    cpt = N // R
                    for ti in range(st["T"] // N):
                        t0 = ti * N
                        c0 = ti * cpt
                        cb = cpt
                        pint, Bpint = nps()
                        for tv in range(R):
                            n_s = R - tv
                            if fwd:
                                rhs = V(zs_all, 0, 128, kt * TALL + zoff + t0, [[R, cpt], [1, n_s]])
                                out = V(pint, 0, 128, tv, [[R, cpt], [1, n_s]])
                            else:
                                rhs = V(zs_all, 0, 128, kt * TALL + zoff + t0 + tv, [[R, cpt], [1, n_s]])
                                out = V(pint, 0, 128, 0, [[R, cpt], [1, n_s]])
                            mm(out, Kk[:, tv, :], rhs, tv == 0, tv == R - 1, r=[B_K, B_zs[sn][ti]], w=[Bpint])
                        k0 = c0 if fwd else (Nc - c0 - cb)
                        for q in range(4):
                            p, Bp = nps()
                            mm(p[0:cb, 0:512], Hk[:, q, 0, hbase + k0:hbase + k0 + cb], Qk[:, q, 0, :, :].rearrange("p a b -> p (a b)"),
                               True, False, r=[B_H, B_Q], w=[Bp])
                            mm(p[0:cb, 0:512], Hk[:, q, 1, hbase + k0:hbase + k0 + cb], Qk[:, q, 1, :, :].rearrange("p a b -> p (a b)"),
                               False, True, r=[B_H, B_Q], w=[Bp])
                            I(act, lambda: nc.scalar.copy(out=YI[0:cb, :, q, :], in_=p[0:cb, 0:512].rearrange("p (s m) -> p s m", m=32)),
                              r=[Bp], w=[B_YI])
                        for sg_ in range(2):
                            pb_, Bpb = npsb()
                            for s8 in range(8):
                                s_ = sg_ * 8 + s8
                                I(pe, lambda: nc.tensor.transpose(
                                    out=pb_[:, s8 * 128:s8 * 128 + cb],
                                    in_=YI[0:cb, s_, :, :].rearrange("p a b -> p (a b)"),
                                    identity=identb[0:cb, 0:cb]), r=[B_YI, B_cst], w=[Bpb], rawself=False)
                            if fwd:
                                src = V(pb_, 0, 128, 0, [[1, cb], [128, 8]])
                            else:
                                src = V(pb_, 0, 128, cb - 1, [[-1, cb], [128, 8]])
                            dst = ystage[:, 0:cb * R].rearrange("p (c s) -> p c s", s=R)[:, :, sg_ * 8:sg_ * 8 + 8]
                            I(dve, lambda: nc.vector.tensor_copy(out=dst, in_=src), r=[Bpb], w=[B_yst])
                        yv = y_all[:, kt, zoff + t0:zoff + t0 + N]
                        if fwd:
                            vv(yv, pint[:, 0:N], ystage[:, 0:N], ALU.add, [Bpint, B_yst], [B_y[sn][ti]])
                        else:
                            vv(ystage[:, 0:N], pint[:, 0:N], ystage[:, 0:N], ALU.add, [Bpint, B_yst], [B_yst])
                            vv(yv, yv, ystage[:, 0:N], ALU.add, [B_yst, B_y[sn][ti]], [B_y[sn][ti]])

        phase_barrier(0)
        for sn in ("ctx", "lat"):
            if sn == "ctx" and lastl:
                continue
            st = streams[sn]
            N, sidx = st["N"], st["sidx"]
            ntile = st["T"] // N
            for ti in range(ntile):
                t0 = ti * N
                z0 = st["off"] + t0
                a0 = st["aoff"] + t0
                load_x(sn, ti, co if sn == "ctx" else xo)
                nb = [B_ad[sn][ti]] + ([B_ad[sn][ti - 1]] if ti > 0 else []) + ([B_ad[sn][ti + 1]] if ti + 1 < ntile else []) + [B_apad]
                dma(sp, a_ld[:, :, 0:N + 30], a_dr[:, :, a0 - 15:a0 + N + 15], r=nb, w=[B_ald])
                yc = mergedb[:, 0:4, :]
                y2 = mergedb[:, 4:8, :]
                for kt in range(4):
                    t, Bt = ntmp()
                    I(dve, lambda: nc.vector.scalar_tensor_tensor(out=t[:, 0:N], in0=zs_all[:, kt, z0:z0 + N],
                                                                  scalar=vecs[:, V_SSMD + kt:V_SSMD + kt + 1],
                                                                  in1=y_all[:, kt, z0:z0 + N], op0=ALU.mult, op1=ALU.add),
                      r=[B_zs[sn][ti], B_y[sn][ti], B_vecs], w=[Bt])
                    I(act, lambda: nc.scalar.activation(out=yc[:, kt, 0:N], in_=t[:, 0:N], func=AF.Gelu_apprx_tanh),
                      r=[Bt], w=[B_mergedb])
                rms_mod(N, 0, 1, sidx)
                hr = lambda kt: hb[:, kt, 0:N]
                ub, vn, sgo = b4[0], b4[1], b4[2]
                B_ub, B_vn, B_sgo = B_b4
                wsu, Bwsu = wload_sc(par, GI_WIN + G_ZB_U, 4096)
                wsv, Bwsv = wload_sc(par, GI_WIN + G_ZB_V, 4096)
                for j in range(4):
                    p, Bp = proj(wsu, Bwsu, j, KT, hr, [B_h], N)
                    I(act, lambda: nc.scalar.activation(out=ub[:, j, 0:N], in_=p[:, 0:N], func=AF.Gelu_apprx_tanh,
                                                        bias=vecs[:, V_BIN + 8 + j:V_BIN + 9 + j], scale=1.0), r=[Bp, B_vecs], w=[B_ub])
                    p, Bp = proj(wsv, Bwsv, j, KT, hr, [B_h], N)
                    I(act, lambda: nc.scalar.activation(out=tmp4[:, j, 0:N], in_=p[:, 0:N], func=AF.Gelu_apprx_tanh,
                                                        bias=vecs[:, V_BIN + 12 + j:V_BIN + 13 + j], scale=1.0), r=[Bp, B_vecs], w=[B_tmp4])

                ln_state = {}

                def ln4_stats():
                    I(act, lambda: nc.scalar.activation(out=sqb[:, 0:4, 0:N], in_=tmp4[:, :, 0:N], func=AF.Square), r=[B_tmp4], w=[B_sqb])
                    I(act, lambda: nc.scalar.copy(out=sqb[:, 4:8, 0:N], in_=tmp4[:, :, 0:N]), r=[B_tmp4], w=[B_sqb])
                    pm_, Bpm_ = nps()
                    pq_, Bpq_ = nps()
                    for kt in range(4):
                        mm(pm_[:, 0:N], ones5[:, :], sqb[:, 4 + kt, 0:N], kt == 0, kt == 3, r=[B_sqb, B_cst], w=[Bpm_])
                    for kt in range(4):
                        mm(pq_[:, 0:N], ones5[:, :], sqb[:, kt, 0:N], kt == 0, kt == 3, r=[B_sqb, B_cst], w=[Bpq_])
                    mean, Bmean = ntmp()
                    I(act, lambda: nc.scalar.copy(out=mean[:, 0:N], in_=pm_[:, 0:N]), r=[Bpm_], w=[Bmean])
                    var, Bvar = ntmp()
                    vv(var[:, 0:N], mean[:, 0:N], mean[:, 0:N], ALU.mult, [Bmean], [Bvar])
                    vv(var[:, 0:N], pq_[:, 0:N], var[:, 0:N], ALU.subtract, [Bpq_, Bvar], [Bvar])
                    I(act, lambda: nc.scalar.activation(out=rstd[:, 0:N], in_=var[:, 0:N], func=AF.Sqrt, bias=epsb[:, 0:1], scale=1.0),
                      r=[Bvar, B_cst], w=[B_rstd])
                    I(dve, lambda: nc.vector.reciprocal(out=rstd[:, 0:N], in_=rstd[:, 0:N]), r=[B_rstd], w=[B_rstd])
                    I(dve, lambda: nc.vector.tensor_copy(out=var[:, 0:N], in_=mean[:, 0:N]), r=[Bmean], w=[Bvar])
                    ln_state["mean"] = (var, Bvar)

                def ln4_norm(kt, gcol, bcol, dst, Bdst, func):
                    mean, Bmean = ln_state["mean"]
                    vv(tmp4[:, kt, 0:N], tmp4[:, kt, 0:N], mean[:, 0:N], ALU.subtract, [B_tmp4, Bmean], [B_tmp4])
                    vv(tmp4[:, kt, 0:N], tmp4[:, kt, 0:N], rstd[:, 0:N], ALU.mult, [B_tmp4, B_rstd], [B_tmp4])
                    I(act, lambda: nc.scalar.activation(out=dst[:, kt, 0:N], in_=tmp4[:, kt, 0:N], func=func,
                                                        bias=vecs[:, bcol + kt:bcol + kt + 1], scale=vecs[:, gcol + kt:gcol + kt + 1]),
                      r=[B_tmp4, B_vecs], w=[Bdst])

                def layernorm4(gcol, bcol, dst, Bdst, func):
                    ln4_stats()
                    for kt in range(4):
                        ln4_norm(kt, gcol, bcol, dst, Bdst, func)
                def branch_out(wd, src, Bsrc, nk, g0, bcolout, mode, between=None):
                    wsb, Bwsb = wload_sc(par, wd, 4096)
                    for gg in range(2):
                        wsg_, Bwsg_ = wload_sc(par, GI_WIN + g0 + gg, 4096)
                        for jj in range(4):
                            mt = gg * 4 + jj
                            pg, Bpg = proj(wsg_, Bwsg_, jj, KT, hr, [B_h], N)
                            gt, Bgt = ngate()
                            bi = WIN_ORDER[(g0 + gg) * 4 + jj]
                            I(act, lambda: nc.scalar.activation(out=gt[:, 0:N], in_=pg[:, 0:N], func=AF.Sigmoid,
                                                                bias=vecs[:, V_BIN + bi:V_BIN + bi + 1], scale=1.0),
                              r=[Bpg, B_vecs], w=[Bgt])
                            po, Bpo = proj(wsb, Bwsb, mt, nk, lambda kt: src[:, kt, 0:N], [Bsrc], N)
                            if mode == 0:
                                I(dve, lambda: nc.vector.scalar_tensor_tensor(out=big1[:, mt, 0:N], in0=po[:, 0:N],
                                                                              scalar=vecs[:, bcolout + mt:bcolout + mt + 1],
                                                                              in1=gt[:, 0:N], op0=ALU.add, op1=ALU.mult),
                                  r=[Bpo, B_vecs, Bgt], w=[B_big1])
                            else:
                                t, Bt = ntmp()
                                I(dve, lambda: nc.vector.scalar_tensor_tensor(out=t[:, 0:N], in0=po[:, 0:N],
                                                                              scalar=vecs[:, bcolout + mt:bcolout + mt + 1],
                                                                              in1=gt[:, 0:N], op0=ALU.add, op1=ALU.mult),
                                  r=[Bpo, B_vecs, Bgt], w=[Bt])
                                if mode == 1:
                                    vv(big1[:, mt, 0:N], big1[:, mt, 0:N], t[:, 0:N], ALU.add, [B_big1, Bt], [B_big1])
                                else:
                                    vv(mergedb[:, mt, 0:N], big1[:, mt, 0:N], t[:, 0:N], ALU.add, [B_big1, Bt], [B_mergedb])
                            if between is not None:
                                between(mt)
                ln4_stats()
                wsq, Bwsq = wload_sc(par, GI_WGLU, 2048)
                for j in range(4):
                    p, Bp = proj(wsq, Bwsq, j, 4, lambda kt: yc[:, kt, 0:N], [B_mergedb], N)
                    gt, Bgt = ngate()
                    I(act, lambda: nc.scalar.activation(out=gt[:, 0:N], in_=p[:, 0:N], func=AF.Sigmoid,
                                                        bias=vecs[:, V_BGLU + j:V_BGLU + j + 1], scale=1.0), r=[Bp, B_vecs], w=[Bgt])
                    vv(y2[:, j, 0:N], yc[:, j, 0:N], gt[:, 0:N], ALU.mult, [B_mergedb, Bgt], [B_mergedb])

                def _between_c(mt):
                    if mt % 2 == 1:
                        ln4_norm(mt // 2, V_SLNG, V_SLNB, vn, B_vn, AF.Identity)
                branch_out(GI_WSS, y2, B_mergedb, 4, G_ZG_C, V_SSBO, 0, between=_between_c)
                nch = N // 128
                for kt in range(4):
                    p, Bp = nps()
                    pb_, Bpb = npsb()
                    for cb_ in range(nch):
                        I(pe, lambda: nc.tensor.transpose(out=pb_[:, cb_ * 128:(cb_ + 1) * 128], in_=vn[:, kt, cb_ * 128:(cb_ + 1) * 128],
                                                          identity=identb[:, :]), r=[B_vn, B_cst], w=[Bpb], rawself=False)
                    I(act, lambda: nc.scalar.copy(out=vT[:, 0:nch, :].rearrange("p c m -> p (c m)"), in_=pb_[:, 0:nch * 128]),
                      r=[Bpb], w=[B_vT])
                    for cb_ in range(nch):
                        for hh in range(2):
                            mm(p[64 * hh:64 * hh + 64, cb_ * 128:(cb_ + 1) * 128], vT[:, cb_, 64 * hh:64 * hh + 64],
                               swT[:, 2 * kt + hh, :], True, True, r=[B_vT, B_swT], w=[Bp], tp=(0, 64 * hh))
                    t, Bt = ntmp()
                    I(dve, lambda: nc.vector.tensor_tensor(out=t[:, 0:N].rearrange("p (c q) -> p c q", q=128),
                                                           in0=p[:, 0:N].rearrange("p (c q) -> p c q", q=128),
                                                           in1=sgb[:, kt, :].unsqueeze(1).to_broadcast([128, nch, 128]), op=ALU.add),
                      r=[Bp, B_sgb], w=[Bt])
                    vv(sgo[:, kt, 0:N], t[:, 0:N], ub[:, kt, 0:N], ALU.mult, [Bt, B_ub], [B_sgo])

                branch_out(GI_WSO, sgo, B_sgo, 4, G_ZG_B, V_SBO, 1)
                ac = b4[0]; B_ac = B_b4[0]
                for kt in range(4):
                    dg, Bdg = (diag, B_diag) if kt % 2 == 0 else (diag2, B_diag2)
                    for kk in range(31):
                        I(dve, lambda: nc.vector.tensor_scalar(out=dg[:, kk, :], in0=identb[:, :], scalar1=cw[:, kt, kk:kk + 1],
                                                               scalar2=None, op0=ALU.mult), r=[B_cst, B_cw], w=[Bdg])
                    p, Bp = nps()
                    for kk in range(31):
                        mm(p[:, 0:N], dg[:, kk, :], a_ld[:, kt, kk:kk + N], kk == 0, kk == 30, r=[Bdg, B_ald], w=[Bp])
                    I(act, lambda: nc.scalar.activation(out=tmp4[:, kt, 0:N], in_=p[:, 0:N], func=AF.Identity,
                                                        bias=vecs[:, V_CONVB + kt:V_CONVB + kt + 1], scale=1.0), r=[Bp, B_vecs], w=[B_tmp4])
                layernorm4(V_CLNG, V_CLNB, ac, B_ac, AF.Silu)
                branch_out(GI_WCO, ac, B_ac, 4, G_ZG_A, V_CBO, 2)
                for gg in range(2):
                    wsw, Bwsw = wload_sc(par, GI_WO + gg, 4096)
                    for jj in range(4):
                        mt = gg * 4 + jj
                        p, Bp = proj(wsw, Bwsw, jj, KT, lambda kt: mergedb[:, kt, 0:N], [B_mergedb], N)
                        I(act, lambda: nc.scalar.activation(out=big1[:, mt, 0:N], in_=p[:, 0:N], func=AF.Identity,
                                                            bias=vecs[:, V_BO + mt:V_BO + mt + 1], scale=1.0), r=[Bp, B_vecs], w=[B_big1])
                resid_update(big1, B_big1, N, 2, sidx)
                rms_mod(N, 3, 4, sidx)
                for g in range(11):
                    wsf, Bwsf = wload_sc(par, GI_WUP + g, 4096)
                    for jj in range(2):
                        j = g * 2 + jj
                        pg, Bpg = proj(wsf, Bwsf, 2 * jj, KT, hr, [B_h], N)
                        pu, Bpu = proj(wsf, Bwsf, 2 * jj + 1, KT, hr, [B_h], N)
                        t, Bt = ntmp()
                        I(act, lambda: nc.scalar.activation(out=t[:, 0:N], in_=pg[:, 0:N], func=AF.Silu), r=[Bpg], w=[Bt])
                        vv(actb[:, j, 0:N], t[:, 0:N], pu[:, 0:N], ALU.mult, [Bt, Bpu], [B_act])
                for mt in range(8):
                    wsd, Bwsd = wload_sc(par, GI_WDN + mt, 2816)
                    p, Bp = proj(wsd, Bwsd, 0, FT, lambda kt: actb[:, kt, 0:N], [B_act], N)
                    I(act, lambda: nc.scalar.copy(out=big1[:, mt, 0:N], in_=p[:, 0:N]), r=[Bp], w=[B_big1])
                resid_update(big1, B_big1, N, 5, sidx)
                store_x(sn, ti, co if sn == "ctx" else xo)

    k.final_wait(sp, [b for bl in B_xd.values() for b in bl])
    print("instructions emitted:", k.nins, "per-engine", k.ncnt, "waits", k.nwait, "dmas", {q.name: q.ndma for q in (k.pool, k.sp)})
    return nc


def tile_w(w, group):
    Kd, M = w.shape
    kt = Kd // 128
    G = M // 128 // group
    w5 = w.reshape(kt, 128, G, group, 128)
    return np.ascontiguousarray(w5.transpose(2, 1, 3, 0, 4)).reshape(G, 128, group * kt * 128)


def fm(v, nt):
    return np.ascontiguousarray(v.reshape(nt, 128).T)


def make_consts():
    c = np.zeros((128, NCST), np.float32)
    c[:, C_ID:C_ID + 128] = np.eye(128, dtype=np.float32)
    blk = np.arange(128) // 16
    c[:, C_KM:C_KM + 128] = (blk[:, None] == blk[None, :]).astype(np.float32)
    gp = np.arange(128) // 64
    c[:, C_GPM + 0] = (gp == 0)
    c[:, C_GPM + 1] = (gp == 1)
    c[:, C_TAU:C_TAU + 17] = np.arange(17, dtype=np.float32)[None, :]
    c[:, C_KIDX:C_KIDX + NKIDX] = np.arange(NKIDX, dtype=np.float32)[None, :]
    c[:, C_POS:C_POS + 64] = np.arange(64, dtype=np.float32)[None, :]
    half = 256
    for kt in range(4):
        dd = kt * 128 + np.arange(128)
        j = dd % half
        om = (1.0 / (10000.0 ** (j.astype(np.float32) / np.float32(half)))).astype(np.float32)
        c[:, C_OM + kt] = om
        c[:, C_PH + kt] = np.where(dd >= half, np.float32(math.pi / 2), np.float32(0.0))
    return c


def prep_shared(inp):
    L = DEPTH
    f = lambda a: np.asarray(a, dtype=np.float32)
    sh = {}
    vecs = np.zeros((L, 128, NV), np.float32)
    for l in range(L):
        vecs[l, :, V_BIN:V_BIN + 44] = fm(f(inp["b_in"][l]), 44)
        vecs[l, :, V_CONVB:V_CONVB + 4] = fm(f(inp["conv_b"][l]), 4)
        vecs[l, :, V_CLNG:V_CLNG + 4] = fm(f(inp["conv_ln_g"][l]), 4)
        vecs[l, :, V_CLNB:V_CLNB + 4] = fm(f(inp["conv_ln_b"][l]), 4)
        vecs[l, :, V_SLNG:V_SLNG + 4] = fm(f(inp["sgu_ln_g"][l]), 4)
        vecs[l, :, V_SLNB:V_SLNB + 4] = fm(f(inp["sgu_ln_b"][l]), 4)
        vecs[l, :, V_SSMD:V_SSMD + 4] = fm(f(inp["ssm_d"][l]), 4)
        vecs[l, :, V_BGLU:V_BGLU + 4] = fm(f(inp["ssm_b_glu"][l]), 4)
        vecs[l, :, V_CBO:V_CBO + 8] = fm(f(inp["conv_b_out"][l]), 8)
        vecs[l, :, V_SBO:V_SBO + 8] = fm(f(inp["sgu_b_out"][l]), 8)
        vecs[l, :, V_SSBO:V_SSBO + 8] = fm(f(inp["ssm_b_out"][l]), 8)
        vecs[l, :, V_BO:V_BO + 8] = fm(f(inp["b_o"][l]), 8)
        for i in range(4):
            vecs[l, :, V_NG + 8 * i:V_NG + 8 * i + 8] = fm(f(inp["norm_g"][l, i]), 8)
        vecs[l, :, V_BADA:V_BADA + 48] = fm(f(inp["b_ada"][l]), 48)
    sh["vecs"] = vecs
    ssmp = np.zeros((L, 2, 128, 1072), np.float32)

    def sm_gp(a):
        return np.ascontiguousarray(a.reshape(16, 2, 64).transpose(1, 2, 0)).reshape(128, 16)

    def sm_gpi(a):
        return np.ascontiguousarray(a.reshape(16, 2, 64, 16).transpose(1, 2, 0, 3)).reshape(128, 256)
    for l in range(L):
        for d in range(2):
            ssmp[l, d, :, 0:16] = sm_gp(f(inp["ssm_lam_re"][l, d]))
            ssmp[l, d, :, 16:32] = sm_gp(f(inp["ssm_lam_im"][l, d]))
            ssmp[l, d, :, 32:48] = sm_gp(np.repeat(f(inp["ssm_log_dt"][l, d])[:, None], 64, axis=1))
            ssmp[l, d, :, 48:304] = sm_gpi(f(inp["ssm_b_re"][l, d]))
            ssmp[l, d, :, 304:560] = sm_gpi(f(inp["ssm_b_im"][l, d]))
            ssmp[l, d, :, 560:816] = sm_gpi(f(inp["ssm_c_re"][l, d]).transpose(0, 2, 1))
            ssmp[l, d, :, 816:1072] = sm_gpi(f(inp["ssm_c_im"][l, d]).transpose(0, 2, 1))
    sh["ssmp"] = ssmp
    sgu_w = f(inp["sgu_w"])
    sh["swT"] = np.ascontiguousarray(sgu_w.transpose(0, 3, 1, 2)).reshape(L, 128, 1024)
    sgu_b = f(inp["sgu_b"])
    sgb = np.zeros((L, 128, 4, 128), np.float32)
    for kt in range(4):
        sgb[:, 0:64, kt, :] = sgu_b[:, 2 * kt, None, :]
        sgb[:, 64:128, kt, :] = sgu_b[:, 2 * kt + 1, None, :]
    sh["sgb"] = sgb.reshape(L, 128, 512)
    conv_w = f(inp["conv_w"])
    sh["cw"] = np.ascontiguousarray(conv_w.reshape(L, 31, 4, 128).transpose(0, 3, 2, 1)).reshape(L, 128, 124)
    sh["wada"] = np.stack([tile_w(f(inp["w_ada"][l]), 4) for l in range(L)])
    perm = np.concatenate([np.arange(m * 128, (m + 1) * 128) for m in WIN_ORDER])
    sh["win"] = np.stack([tile_w(f(inp["w_in"][l])[:, perm], 4) for l in range(L)])
    sh["wco"] = np.stack([tile_w(f(inp["conv_w_out"][l]), 8)[0] for l in range(L)])
    sh["wso"] = np.stack([tile_w(f(inp["sgu_w_out"][l]), 8)[0] for l in range(L)])
    sh["wss"] = np.stack([tile_w(f(inp["ssm_w_out"][l]), 8)[0] for l in range(L)])
    sh["wglu"] = np.stack([tile_w(f(inp["ssm_w_glu"][l]), 4)[0] for l in range(L)])
    sh["wo"] = np.stack([tile_w(f(inp["w_o"][l]), 4) for l in range(L)])
    upo = []
    for j in range(FT):
        upo += [j, FT + j]
    permu = np.concatenate([np.arange(m * 128, (m + 1) * 128) for m in upo])
    sh["wup"] = np.stack([tile_w(f(inp["ffn_w_up"][l])[:, permu], 4) for l in range(L)])
    sh["wdn"] = np.stack([tile_w(f(inp["ffn_w_down"][l]), 1) for l in range(L)])
    sh["cst"] = make_consts()
    return sh


_NC_CACHE = {}


def kernel(**inputs):
    x = np.asarray(inputs["x"], np.float32)
    c = np.asarray(inputs["c"], np.float32)
    ctx = np.asarray(inputs["ctx"], np.float32)
    c_ctx = np.asarray(inputs["c_ctx"], np.float32)
    sh = prep_shared(inputs)
    ncores = 8
    per_core = []
    for core in range(ncores):
        b = core % 4
        m = dict(sh)
        m["xT"] = np.ascontiguousarray(x[b].T).reshape(KT, 128, T_LAT)
        m["ctxT"] = np.ascontiguousarray(ctx[b].T).reshape(KT, 128, T_CTX)
        sc = np.zeros((128, KT, 2), np.float32)
        sc[:, :, 0] = fm(c[b], KT)
        sc[:, :, 1] = fm(c_ctx, KT)
        m["sc_in"] = sc.reshape(128, KT * 2)
        per_core.append(m)
    if FUSED:
        plan = [(list(range(DEPTH)), True)]
    else:
        plan = [([l], l == 0) for l in range(DEPTH)]
    res = None
    for (layers, first) in plan:
        key = (len(layers), first, layers[-1] == DEPTH - 1, layers[0] if len(layers) > 1 else -1)
        if key not in _NC_CACHE:
            _NC_CACHE[key] = build(layers, first, layers[-1] == DEPTH - 1)
        nc = _NC_CACHE[key]
        LKEYS = ("vecs", "ssmp", "swT", "sgb", "cw", "wada", "win", "wco", "wso", "wss", "wglu", "wo", "wup", "wdn")
        l0, l1 = layers[0], layers[-1] + 1
        launch_maps = []
        for core in range(ncores):
            m = dict(per_core[core])
            for kk in LKEYS:
                m[kk] = np.ascontiguousarray(sh[kk][l0:l1])
            launch_maps.append(m)
        res = run_bass_kernel_spmd(nc, launch_maps, core_ids=list(range(ncores)))
        if not FUSED:
            for core in range(ncores):
                per_core[core]["xT"] = res.results[core]["xo"]
                per_core[core]["ctxT"] = res.results[core]["co"]
    out = np.zeros((4, T_LAT, D), np.float32)
    for b in range(4):
        out[b] = res.results[b]["xo"].reshape(D, T_LAT).T
    return out
```

```python
import math
import numpy as np
import concourse.bass as bass
import concourse.mybir as mybir
from concourse.bass_utils import run_bass_kernel_spmd

F32 = mybir.dt.float32
BF16 = mybir.dt.bfloat16
AF = mybir.ActivationFunctionType
ALU = mybir.AluOpType

FUSED = True
DEPTH = 4
D = 1024
KT = 8
WBR = 512
T_LAT = 4096
T_CTX = 256
TT = 512
R = 16
DFF = 2816
FT = 22
EPS = 1e-6
TWO_PI = 2.0 * math.pi
NV = 184
V_BIN, V_CONVB, V_CLNG, V_CLNB, V_SLNG, V_SLNB, V_SSMD, V_BGLU = 0, 44, 48, 52, 56, 60, 64, 68
V_CBO, V_SBO, V_SSBO, V_BO, V_NG, V_BADA = 72, 80, 88, 96, 104, 136
C_ID, C_KM, C_GPM, C_TAU, C_KIDX, C_POS, C_OM, C_PH = 0, 128, 256, 258, 275, 535, 599, 603
NCST = 607
NKIDX = 260
WIN_ORDER = [4, 5, 6, 7, 0, 1, 2, 3, 16, 17, 18, 19, 8, 9, 10, 11, 12, 13, 14, 15,
             28, 29, 30, 31, 32, 33, 34, 35, 20, 21, 22, 23, 24, 25, 26, 27, 36, 37, 38, 39, 40, 41, 42, 43]
G_ZA_GATE, G_ZA_A, G_ZS, G_ZB_U, G_ZB_V, G_ZG_B, G_ZG_A, G_ZG_C = 0, 1, 2, 3, 4, 5, 7, 9


class Buf:
    def __init__(self, name):
        self.name = name
        self.w = {}
        self.r = {}
        self.alias = []


class Eng:
    def __init__(self, nc, name, h):
        self.nc, self.name, self.h = nc, name, h
        self.n = 0
        self.gen = 0
        self.sem = nc.alloc_semaphore(name=f"{name}_p0")
        self.semname = f"{name}_p0"
        self.seen = {}
        self.ndma = 0
        self.dsems = None

    def roll(self):
        if self.n >= 30000:
            self.gen += 1
            self.semname = f"{self.name}_p{self.gen}"
            self.sem = self.nc.alloc_semaphore(name=self.semname)
            self.n = 0


class K:
    def __init__(self, nc):
        self.nc = nc
        self.pe = Eng(nc, "pe", nc.tensor)
        self.act = Eng(nc, "act", nc.scalar)
        self.dve = Eng(nc, "dve", nc.vector)
        self.pool = Eng(nc, "pool", nc.gpsimd)
        self.sp = Eng(nc, "sp", nc.sync)
        for q in (self.pool, self.sp):
            q.dsems = [(nc.alloc_semaphore(name=f"{q.name}_d{i}"), f"{q.name}_d{i}") for i in range(8)]
        self.nins = 0
        self.nwait = {}
        self.ncnt = {}

    def _wait(self, eng, toks):
        for (sem, semname, val, _o) in toks:
            if eng.seen.get(semname, 0) >= val:
                continue
            eng.h.wait_ge(sem, val)
            self.nwait[eng.name] = self.nwait.get(eng.name, 0) + 1
            eng.seen[semname] = val

    def _deps(self, eng, r, w, rawself=True):
        toks = []
        for b in r:
            for key, t in b.w.items():
                if key == eng.name and not rawself:
                    continue
                toks.append(t)
        for b0 in w:
            for b in [b0] + b0.alias:
                for key, t in b.w.items():
                    if key != eng.name:
                        toks.append(t)
                for key, t in b.r.items():
                    if key != eng.name:
                        toks.append(t)
        return toks

    def I(self, eng, fn, r=(), w=(), rawself=True, inc=True):
        self._wait(eng, self._deps(eng, r, w, rawself))
        if not getattr(eng, "pending", False):
            eng.roll()
        ins = fn()
        self.ncnt[eng.name] = self.ncnt.get(eng.name, 0) + 1
        if inc:
            eng.n += 1
            ins.then_inc(eng.sem, 1)
            eng.pending = False
            tok = (eng.sem, eng.semname, eng.n, self.nins)
        else:
            eng.pending = True
            tok = (eng.sem, eng.semname, eng.n + 1, self.nins)
        for b in r:
            b.r[eng.name] = tok
        for b0 in w:
            for b in [b0] + b0.alias:
                b.w[eng.name] = tok
        self.nins += 1
        return ins

    def dma(self, q, out, in_, r=(), w=()):
        i = q.ndma
        sem, semname = q.dsems[i % 8]
        prev = 16 * (i // 8)
        if prev > 0:
            self._wait(q, [(sem, semname, prev, 0)])
        self._wait(q, self._deps(q, r, w))
        q.h.dma_start(out=out, in_=in_).then_inc(sem, 16)
        q.ndma += 1
        tok = (sem, semname, prev + 16, self.nins)
        for b in r:
            b.r[semname] = tok
        for b0 in w:
            for b in [b0] + b0.alias:
                b.w[semname] = tok
        self.nins += 1
        return tok

    def final_wait(self, q, bufs):
        toks = []
        for b in bufs:
            toks += list(b.w.values())
        self._wait(q, toks)


def V(t, p0, npart, off, dims):
    fs = 1
    for s in t.shape[1:]:
        fs *= s
    return bass.AP(tensor=t, offset=p0 * fs + off, ap=[[fs, npart]] + [list(d) for d in dims])


def build(layers, first, last_out):
    nc = bass.Bass("TRN2", target_bir_lowering=False)
    k = K(nc)
    I, dma = k.I, k.dma
    pe, act, dve, pool, sp = k.pe, k.act, k.dve, k.pool, k.sp
    L = len(layers)

    def din(name, shape, dt=F32):
        return nc.dram_tensor(name, list(shape), dt, kind="ExternalInput").ap()

    xT = din("xT", [KT, 128, T_LAT])
    ctxT = din("ctxT", [KT, 128, T_CTX])
    sc_in = din("sc_in", [128, KT * 2])
    vecs_d = din("vecs", [L, 128, NV])
    ssmp_d = din("ssmp", [L, 2, 128, 1072])
    swT_d = din("swT", [L, 128, 1024])
    sgb_d = din("sgb", [L, 128, 512])
    cw_d = din("cw", [L, 128, 124])
    wada_d = din("wada", [L, 12, 128, 4096])
    win_d = din("win", [L, 11, 128, 4096])
    wco_d = din("wco", [L, 128, 4096])
    wso_d = din("wso", [L, 128, 4096])
    wss_d = din("wss", [L, 128, 4096])
    wglu_d = din("wglu", [L, 128, 2048])
    wo_d = din("wo", [L, 2, 128, 4096])
    wup_d = din("wup", [L, 11, 128, 4096])
    wdn_d = din("wdn", [L, 8, 128, 2816])
    cst_d = din("cst", [128, NCST])
    xo = nc.dram_tensor("xo", [KT, 128, T_LAT], F32, kind="ExternalOutput").ap()
    co = nc.dram_tensor("co", [KT, 128, T_CTX], F32, kind="ExternalOutput").ap()
    TALL = T_CTX + T_LAT
    APAD = 16
    AW = TALL + 4 * APAD
    a_dr = nc.dram_tensor("a_dr", [128, 4, AW], BF16).ap()
    NGRP = 36
    wsc = nc.dram_tensor("wsc", [2, NGRP, 128, 4096], BF16).ap()
    B_wsc = [[Buf(f"wsc{p_}_{g_}") for g_ in range(NGRP)] for p_ in range(2)]
    GI_WIN, GI_WCO, GI_WSO, GI_WSS, GI_WGLU, GI_WO, GI_WUP, GI_WDN = 0, 11, 12, 13, 14, 15, 17, 28

    def sb(name, shape, dt=F32):
        return nc.alloc_sbuf_tensor("s_" + name, list(shape), dt)

    cst = sb("cst", [128, NCST]); B_cst = Buf("cst")
    identb = sb("identb", [128, 128], BF16)
    onesb = sb("onesb", [128, 128], BF16)
    ones5 = sb("ones5", [128, 128], BF16)
    negpi = sb("negpi", [128, 1])
    kmask = cst[:, C_KM:C_KM + 128]
    emb = sb("emb", [128, 4, 64]); B_emb = Buf("emb")
    scb = sb("scb", [128, KT * 2], BF16); B_scb = Buf("scb")
    vecs = sb("vecs", [128, NV]); B_vecs = Buf("vecs")
    modv = sb("modv", [128, 48, 2]); B_mod = Buf("mod")
    gs = sb("gs", [128, 6, KT, 2]); B_gs = Buf("gs")
    swT = sb("swT", [128, 8, 128], BF16); B_swT = Buf("swT")
    sgb = sb("sgb", [128, 4, 128]); B_sgb = Buf("sgb")
    cw = sb("cw", [128, 4, 31]); B_cw = Buf("cw")
    zs_all = sb("zs_all", [128, 4, TALL], BF16)
    y_all = sb("y_all", [128, 4, TALL], BF16)
    streams = {"ctx": dict(T=T_CTX, off=0, aoff=APAD, N=256, sidx=1),
               "lat": dict(T=T_LAT, off=T_CTX, aoff=T_CTX + 3 * APAD, N=TT, sidx=0)}
    B_zs = {}; B_y = {}; B_ad = {}
    for sn, st in streams.items():
        nt = st["T"] // st["N"]
        B_zs[sn] = [Buf(f"zs{sn}{i}") for i in range(nt)]
        B_y[sn] = [Buf(f"y{sn}{i}") for i in range(nt)]
        B_ad[sn] = [Buf(f"ad{sn}{i}") for i in range(nt)]
    B_apad = Buf("apad")
    rstd = sb("rstd", [128, TT]); B_rstd = Buf("rstd")
    tmpf = [sb(f"tmpf{i}", [128, TT]) for i in range(2)]; B_tmpf = [Buf(f"tmpf{i}") for i in range(2)]
    tctr = [0]

    def ntmp():
        i = tctr[0] % 2
        tctr[0] += 1
        return tmpf[i], B_tmpf[i]
    gateb = [sb(f"gateb{i}", [128, TT], BF16) for i in range(2)]; B_gate = [Buf(f"gate{i}") for i in range(2)]
    gctr = [0]

    def ngate():
        i = gctr[0] % 2
        gctr[0] += 1
        return gateb[i], B_gate[i]

    NPS = 6
    psf = [nc.alloc_psum_tensor(f"psf{i}", [128, 512], F32) for i in range(NPS)]
    B_psf = [Buf(f"psf{i}") for i in range(NPS)]
    psb = [nc.alloc_psum_tensor(f"psb{i}", [128, 1024], BF16) for i in range(2)]
    B_psb = [Buf(f"psb{i}") for i in range(2)]
    pctr = [0, 0]

    def nps():
        i = pctr[0] % NPS
        pctr[0] += 1
        return psf[i], B_psf[i]

    def npsb():
        i = pctr[1] % 2
        pctr[1] += 1
        return psb[i], B_psb[i]

    UB = 121 * 1024
    uni = sb("uni", [128, UB // 2], BF16)
    B_AC_all = []
    B_SSM_all = []
    ucur = [0]

    UFLAT = {}

    def review(name, shape, dt):
        o_el, nb = UFLAT[name]
        n = 1
        for s_ in shape:
            n *= s_
        assert n * (4 if dt == F32 else 2) <= nb, (name, shape)
        v = uni[:, o_el:o_el + nb // 2]
        if dt == F32:
            v = v.bitcast(F32)
        v = v[:, 0:n]
        if len(shape) == 3:
            v = v.rearrange("p (a b c) -> p a b c", a=shape[0], b=shape[1])
        elif len(shape) == 2:
            v = v.rearrange("p (a b) -> p a b", a=shape[0])
        return v

    def ualloc(shape, dt, side, name):
        n = 1
        for s_ in shape:
            n *= s_
        nb = n * (4 if dt == F32 else 2)
        nb = (nb + 63) // 64 * 64
        o_el = ucur[0] // 2
        ucur[0] += nb
        assert ucur[0] <= UB, f"union overflow {name} {ucur[0]}"
        UFLAT[name] = (o_el, nb)
        v = uni[:, o_el:o_el + nb // 2]
        if dt == F32:
            v = v.bitcast(F32)
        v = v[:, 0:n]
        if len(shape) == 2:
            v = v.rearrange("p (a b) -> p a b", a=shape[0])
        elif len(shape) == 3:
            v = v.rearrange("p (a b c) -> p a b c", a=shape[0], b=shape[1])
        elif len(shape) == 4:
            v = v.rearrange("p (a b c d) -> p a b c d", a=shape[0], b=shape[1], c=shape[2])
        b = Buf(name)
        (B_AC_all if side == 0 else B_SSM_all).append(b)
        return v, b

    ucur[0] = 0
    NSLOT = 4
    wslot = []; B_ws = []
    for i in range(NSLOT):
        v, b = ualloc([4096], BF16, 0, f"ws{i}")
        wslot.append(v); B_ws.append(b)
    xt, B_xt = ualloc([KT, TT], F32, 0, "xt")
    hb, B_h = ualloc([KT, TT], BF16, 0, "h")
    big1, B_big1 = ualloc([KT, TT], F32, 0, "big1")
    act_off = ucur[0]
    mergedb, B_mergedb = ualloc([KT, TT], BF16, 0, "mergedb")
    tmp4, B_tmp4 = ualloc([4, TT], F32, 0, "tmp4")
    b4 = []; B_b4 = []
    for i in range(3):
        v, b = ualloc([4, TT], BF16, 0, f"b4{i}")
        b4.append(v); B_b4.append(b)
    pre_end = ucur[0]
    assert pre_end - act_off >= FT * TT * 2
    actb = uni[:, act_off // 2:act_off // 2 + FT * TT].rearrange("p (a b) -> p a b", a=FT)
    B_act = Buf("act"); B_AC_all.append(B_act)
    pre = [B_mergedb, B_tmp4] + B_b4
    B_act.alias = list(pre)
    for b in pre:
        b.alias = [B_act]
    sqb, B_sqb = ualloc([KT, TT], BF16, 0, "sqb")
    diag = review("mergedb", [31, 128], BF16)
    B_diag = Buf("diag"); B_AC_all.append(B_diag)
    B_diag.alias = [B_mergedb]
    B_mergedb.alias = B_mergedb.alias + [B_diag]
    vT, B_vT = ualloc([4, 128], BF16, 0, "vT")
    diag2 = review("sqb", [31, 128], BF16)
    B_diag2 = Buf("diag2"); B_AC_all.append(B_diag2)
    B_diag2.alias = [B_sqb]
    B_sqb.alias = [B_diag2]
    a_ld, B_ald = ualloc([4, TT + 32], BF16, 0, "a_ld")
    side0_end = ucur[0]
    wctr = [0]

    slot_last = [0] * NSLOT

    def pick_slot():
        i = min(range(NSLOT), key=lambda j: slot_last[j])
        slot_last[i] = k.nins + 1
        return i

    def touch_slot(Bws):
        for j in range(NSLOT):
            if B_ws[j] is Bws:
                slot_last[j] = k.nins + 1

    def wload(src_ap, nelem):
        i = pick_slot()
        wctr[0] += 1
        dma(pool, wslot[i][:, 0:nelem], src_ap, w=[B_ws[i]])
        return wslot[i], B_ws[i]

    def wload_sc(par, gi, nelem):
        i = pick_slot()
        wctr[0] += 1
        dma(sp, wslot[i][:, 0:nelem], wsc[par, gi, :, 0:nelem], r=[B_wsc[par][gi]], w=[B_ws[i]])
        return wslot[i], B_ws[i]

    def convert_layer(lj, par):
        def cv(gi, src, n):
            dma(pool, wsc[par, gi, :, 0:n], src, w=[B_wsc[par][gi]])
        for g_ in range(11):
            cv(GI_WIN + g_, win_d[lj, g_, :, :], 4096)
        cv(GI_WCO, wco_d[lj, :, :], 4096)
        cv(GI_WSO, wso_d[lj, :, :], 4096)
        cv(GI_WSS, wss_d[lj, :, :], 4096)
        cv(GI_WGLU, wglu_d[lj, :, :], 2048)
        for g_ in range(2):
            cv(GI_WO + g_, wo_d[lj, g_, :, :], 4096)
        for g_ in range(11):
            cv(GI_WUP + g_, wup_d[lj, g_, :, :], 4096)
        for g_ in range(8):
            cv(GI_WDN + g_, wdn_d[lj, g_, :, :], 2816)

    ucur[0] = 0
    NP = 16
    NCM = T_LAT // R
    NCC = T_CTX // R
    HW = (NCC + 1) + (NCM + 1)
    NCW = max(NCM + 1, 136)
    ssmp, B_ssmp = ualloc([1072], F32, 1, "ssmp")
    t17, B_t17 = ualloc([NP, R + 1], F32, 1, "t17")
    magt, B_magt = ualloc([NP, R + 1], F32, 1, "magt")
    sint, B_sint = ualloc([NP, R + 1], F32, 1, "sint")
    cost, B_cost = ualloc([NP, R + 1], F32, 1, "cost")
    ar, B_apow = ualloc([NP, R + 1], F32, 1, "ar")
    ai, _ = ualloc([NP, R + 1], F32, 1, "ai")
    sm, B_sm = ualloc([12, NP], F32, 1, "sm")
    bbr, B_bb = ualloc([NP, 16], F32, 1, "bbr")
    bbi, _ = ualloc([NP, 16], F32, 1, "bbi")
    bbxr, B_bbx = ualloc([NP, 2, 16], BF16, 1, "bbxr")
    bbxi, _ = ualloc([NP, 2, 16], BF16, 1, "bbxi")
    wt = []; B_w = []
    for i in range(4):
        v, b = ualloc([NP, 16], F32, 1, f"w{i}")
        wt.append(v); B_w.append(b)
    xm = []; B_xm = []
    for i in range(2):
        v, b = ualloc([4, 2, 16], BF16, 1, f"xm{i}")
        xm.append(v); B_xm.append(b)
    xmall = []; B_xmall = []
    for i in range(2):
        v, b = ualloc([R + 1, 4, 2, 16], BF16, 1, f"xmall{i}")
        xmall.append(v); B_xmall.append(b)
    Pk, B_P = ualloc([R, 2, 128], BF16, 1, "Pk")
    Qk, B_Q = ualloc([4, 2, R, 32], BF16, 1, "Qk")
    Kk, B_K = ualloc([R, 128], BF16, 1, "Kk")
    SH, B_SH = ualloc([4, 2, NCW], F32, 1, "SH")
    TK, B_TK = ualloc([4, 2, NCW], F32, 1, "TK")
    Marr, B_M = ualloc([4, 2, NCW], F32, 1, "Marr")
    Garr, B_G = ualloc([4, 2, NCW], F32, 1, "Garr")
    cosk, B_ck = ualloc([4, NCM + 1], F32, 1, "cosk")
    sink, _ = ualloc([4, NCM + 1], F32, 1, "sink")
    angk, B_angk = ualloc([4, NCM + 1], F32, 1, "angk")
    Hk, B_H = ualloc([4, 2, HW], BF16, 1, "Hk")
    hfin, B_hfin = ualloc([4, 2], F32, 1, "hfin")
    YI, B_YI = ualloc([R, 4, 32], BF16, 1, "YI")
    NRED = max(4 * (NCM + 1), NP * (R + 1))
    redf, B_red = ualloc([NRED], F32, 1, "redf")
    redi_f, _ = ualloc([NRED], F32, 1, "redi")
    redi = redi_f.bitcast(mybir.dt.int32)
    redf3 = redf[:, 0:4 * (NCM + 1)].rearrange("p (a b) -> p a b", a=4)
    redi3 = redi[:, 0:4 * (NCM + 1)].rearrange("p (a b) -> p a b", a=4)
    ystage, B_yst = ualloc([TT], F32, 1, "ystage")
    side1_end = ucur[0]
    print("union side0", side0_end, "side1", side1_end, "sbuf remaining", nc.sbuf_bytes_remaining)
    SHW = NCM + 1

    def phase_barrier(new_side):
        old = B_SSM_all if new_side == 0 else B_AC_all
        new = B_AC_all if new_side == 0 else B_SSM_all
        merged = {}
        for ob in old:
            for key, t in list(ob.w.items()) + list(ob.r.items()):
                kk = "bar_" + key
                if kk not in merged or merged[kk][3] < t[3]:
                    merged[kk] = t
        for b in new:
            for kk, t in merged.items():
                if kk not in b.r or b.r[kk][3] < t[3]:
                    b.r[kk] = t

    dma(sp, cst[:, :], cst_d[:, :], w=[B_cst])
    I(dve, lambda: nc.vector.tensor_copy(out=identb[:, :], in_=cst[:, C_ID:C_ID + 128]), r=[B_cst], w=[B_cst])
    I(dve, lambda: nc.vector.memset(onesb[:, :], 1.0 / 1024), w=[B_cst])
    I(dve, lambda: nc.vector.memset(ones5[:, :], 1.0 / 512), w=[B_cst])
    I(dve, lambda: nc.vector.memset(negpi[:, :], -math.pi), w=[B_cst])
    epsb = sb("epsb", [128, 1])
    I(dve, lambda: nc.vector.memset(epsb[:, :], EPS), w=[B_cst])
    I(dve, lambda: nc.vector.memset(a_ld[:, :, 0:APAD], 0.0), w=[B_ald])
    for off in (0, APAD + T_CTX, T_CTX + 2 * APAD, T_CTX + 3 * APAD + T_LAT):
        dma(sp, a_dr[:, :, off:off + APAD], a_ld[:, :, 0:APAD], r=[B_ald], w=[B_apad])
    if first:
        I(dve, lambda: nc.vector.tensor_tensor(out=emb[:, :, :], in0=cst[:, C_OM:C_OM + 4].unsqueeze(2).to_broadcast([128, 4, 64]),
                                               in1=cst[:, C_POS:C_POS + 64].unsqueeze(1).to_broadcast([128, 4, 64]), op=ALU.mult),
          r=[B_cst], w=[B_emb])
        I(dve, lambda: nc.vector.tensor_tensor(out=emb[:, :, :], in0=emb[:, :, :],
                                               in1=cst[:, C_PH:C_PH + 4].unsqueeze(2).to_broadcast([128, 4, 64]), op=ALU.add),
          r=[B_cst, B_emb], w=[B_emb])
    sc32 = sb("sc32", [128, KT * 2])
    dma(sp, sc32[:, :], sc_in[:, :], w=[B_scb])
    sct = sb("sct", [128, KT * 2])
    I(act, lambda: nc.scalar.activation(out=sct[:, :], in_=sc32[:, :], func=AF.Sigmoid), r=[B_scb], w=[B_scb])
    I(dve, lambda: nc.vector.tensor_tensor(out=scb[:, :], in0=sct[:, :], in1=sc32[:, :], op=ALU.mult), r=[B_scb], w=[B_scb])
    scb3 = scb[:, :].rearrange("p (k s) -> p k s", s=2)

    def mm(out, lhsT, rhs, start, stop, r, w, tp=None):
        if tp is None:
            return I(pe, lambda: nc.tensor.matmul(out, lhsT=lhsT, rhs=rhs, start=start, stop=stop), r=r, w=w, rawself=False,
                     inc=stop)
        return I(pe, lambda: nc.tensor.matmul(out, lhsT=lhsT, rhs=rhs, start=start, stop=stop, tile_position=tp),
                 r=r, w=w, rawself=False, inc=stop)

    def vv(out, in0, in1, op, r, w):
        I(dve, lambda: nc.vector.tensor_tensor(out=out, in0=in0, in1=in1, op=op), r=r, w=w)


    C1_2PI = 6.28125
    C2_2PI = TWO_PI - 6.28125
    PI_LO = 3.1415925

    def reduce_angle(ang, ti, tf, rB, Bang, Bt):
        I(dve, lambda: nc.vector.tensor_scalar(out=tf, in0=ang, scalar1=1.0 / TWO_PI, scalar2=None, op0=ALU.mult),
          r=rB + [Bang], w=[Bt])
        I(dve, lambda: nc.vector.tensor_copy(out=ti, in_=tf), r=[Bt], w=[Bt])
        I(dve, lambda: nc.vector.tensor_copy(out=tf, in_=ti), r=[Bt], w=[Bt])
        I(dve, lambda: nc.vector.scalar_tensor_tensor(out=ang, in0=tf, scalar=-C1_2PI, in1=ang, op0=ALU.mult, op1=ALU.add),
          r=[Bt, Bang], w=[Bang])
        I(dve, lambda: nc.vector.scalar_tensor_tensor(out=ang, in0=tf, scalar=-C2_2PI, in1=ang, op0=ALU.mult, op1=ALU.add),
          r=[Bt, Bang], w=[Bang])
        I(dve, lambda: nc.vector.tensor_scalar(out=ang, in0=ang, scalar1=-PI_LO, scalar2=PI_LO, op0=ALU.max, op1=ALU.min),
          r=[Bang], w=[Bang])

    def sincos(ang, sin_o, cos_o, Bang, Bs, Bc):
        I(act, lambda: nc.scalar.activation(out=sin_o, in_=ang, func=AF.Sin), r=[Bang], w=[Bs])
        I(act, lambda: nc.scalar.activation(out=cos_o, in_=ang, func=AF.Sin, scale=0.5), r=[Bang], w=[Bc])
        I(dve, lambda: nc.vector.tensor_tensor(out=cos_o, in0=cos_o, in1=cos_o, op=ALU.mult), r=[Bc], w=[Bc])
        I(dve, lambda: nc.vector.tensor_scalar(out=cos_o, in0=cos_o, scalar1=-2.0, scalar2=1.0, op0=ALU.mult, op1=ALU.add),
          r=[Bc], w=[Bc])

    def rms_stats(src, Bsrc, N):
        I(act, lambda: nc.scalar.activation(out=sqb[:, :, 0:N], in_=src[:, :, 0:N], func=AF.Square), r=[Bsrc], w=[B_sqb])
        p, Bp = nps()
        for kt in range(KT):
            mm(p[:, 0:N], onesb[:, :], sqb[:, kt, 0:N], kt == 0, kt == KT - 1, r=[B_sqb, B_cst], w=[Bp])
        I(act, lambda: nc.scalar.activation(out=rstd[:, 0:N], in_=p[:, 0:N], func=AF.Sqrt, bias=epsb[:, 0:1], scale=1.0),
          r=[Bp, B_cst], w=[B_rstd])
        I(dve, lambda: nc.vector.reciprocal(out=rstd[:, 0:N], in_=rstd[:, 0:N]), r=[B_rstd], w=[B_rstd])

    def rms_mod(N, gi, si, sidx):
        rms_stats(xt, B_xt, N)
        for kt in range(KT):
            t, Bt = ntmp()
            I(dve, lambda: nc.vector.scalar_tensor_tensor(out=t[:, 0:N], in0=xt[:, kt, 0:N], scalar=gs[:, gi, kt, sidx:sidx + 1],
                                                          in1=rstd[:, 0:N], op0=ALU.mult, op1=ALU.mult),
              r=[B_xt, B_gs, B_rstd], w=[Bt])
            I(act, lambda: nc.scalar.activation(out=hb[:, kt, 0:N], in_=t[:, 0:N], func=AF.Identity,
                                                bias=gs[:, si, kt, sidx:sidx + 1], scale=1.0), r=[Bt, B_gs], w=[B_h])

    def resid_update(src, Bsrc, N, gi, sidx):
        rms_stats(src, Bsrc, N)
        for kt in range(KT):
            t, Bt = ntmp()
            I(dve, lambda: nc.vector.scalar_tensor_tensor(out=t[:, 0:N], in0=src[:, kt, 0:N], scalar=gs[:, gi, kt, sidx:sidx + 1],
                                                          in1=rstd[:, 0:N], op0=ALU.mult, op1=ALU.mult),
              r=[Bsrc, B_gs, B_rstd], w=[Bt])
            I(dve, lambda: nc.vector.tensor_tensor(out=xt[:, kt, 0:N], in0=xt[:, kt, 0:N], in1=t[:, 0:N], op=ALU.add),
              r=[B_xt, Bt], w=[B_xt])

    def proj(ws, Bws, j, nk, rhs_fn, rB, N):
        p, Bp = nps()
        for kt in range(nk):
            o0 = (j * nk + kt) * 128
            mm(p[:, 0:N], ws[:, o0:o0 + 128], rhs_fn(kt), kt == 0, kt == nk - 1, r=[Bws] + rB, w=[Bp])
        touch_slot(Bws)
        return p, Bp

    if first:
        embf = big1[:, 0, 0:256].rearrange("p (a b) -> p a b", a=4)
        embi = big1[:, 1, 0:256].bitcast(mybir.dt.int32).rearrange("p (a b) -> p a b", a=4)
        reduce_angle(emb[:, :, :], embi, embf, [], B_emb, B_big1)
        I(act, lambda: nc.scalar.activation(out=emb[:, :, :], in_=emb[:, :, :], func=AF.Sin), r=[B_emb], w=[B_emb])
    B_xd = {"ctx": [Buf("xdc0")], "lat": [Buf(f"xdl{i}") for i in range(T_LAT // TT)]}

    def load_x(sn, ti, src):
        N = streams[sn]["N"]
        t0 = ti * N
        dma(sp, xt[:, :, 0:N], src[:, :, t0:t0 + N].rearrange("k p t -> p k t"), r=[B_xd[sn][ti]], w=[B_xt])

    def store_x(sn, ti, dst):
        N = streams[sn]["N"]
        t0 = ti * N
        dma(sp, dst[:, :, t0:t0 + N].rearrange("k p t -> p k t"), xt[:, :, 0:N], r=[B_xt], w=[B_xd[sn][ti]])

    for li, l in enumerate(layers):
        from_input = (li == 0)
        add_pe = first and li == 0
        lastl = (l == DEPTH - 1)
        par = li % 2
        if li == 0:
            convert_layer(0, 0)
        if li > 0:
            phase_barrier(0)
        dma(sp, vecs[:, :], vecs_d[li, :, :], w=[B_vecs])
        dma(pool, swT[:, :, :].rearrange("p a b -> p (a b)"), swT_d[li, :, :], w=[B_swT])
        dma(sp, sgb[:, :, :].rearrange("p a b -> p (a b)"), sgb_d[li, :, :], w=[B_sgb])
        dma(sp, cw[:, :, :].rearrange("p a b -> p (a b)"), cw_d[li, :, :], w=[B_cw])
        pm, Bpm = nps()
        for g in range(12):
            ws, Bws = wload(wada_d[li, g, :, :], 4096)
            for j in range(4):
                mt = g * 4 + j
                for kt in range(KT):
                    o0 = (j * KT + kt) * 128
                    mm(pm[:, 2 * mt:2 * mt + 2], ws[:, o0:o0 + 128], scb3[:, kt, :], kt == 0, kt == KT - 1,
                       r=[Bws, B_scb], w=[Bpm])
            touch_slot(Bws)
        I(dve, lambda: nc.vector.tensor_tensor(out=modv[:, :, :], in0=pm[:, 0:96].rearrange("p (m s) -> p m s", s=2),
                                               in1=vecs[:, V_BADA:V_BADA + 48].unsqueeze(2).to_broadcast([128, 48, 2]), op=ALU.add),
          r=[Bpm, B_vecs], w=[B_mod])

        def ng(i):
            return vecs[:, V_NG + 8 * i:V_NG + 8 * i + 8].unsqueeze(2).to_broadcast([128, 8, 2])
        I(dve, lambda: nc.vector.scalar_tensor_tensor(out=gs[:, 0, :, :], in0=modv[:, 8:16, :], scalar=1.0, in1=ng(0),
                                                      op0=ALU.add, op1=ALU.mult), r=[B_mod, B_vecs], w=[B_gs])
        I(dve, lambda: nc.vector.tensor_copy(out=gs[:, 1, :, :], in_=modv[:, 0:8, :]), r=[B_mod], w=[B_gs])
        I(dve, lambda: nc.vector.tensor_tensor(out=gs[:, 2, :, :], in0=modv[:, 16:24, :], in1=ng(1), op=ALU.mult),
          r=[B_mod, B_vecs], w=[B_gs])
        I(dve, lambda: nc.vector.scalar_tensor_tensor(out=gs[:, 3, :, :], in0=modv[:, 32:40, :], scalar=1.0, in1=ng(2),
                                                      op0=ALU.add, op1=ALU.mult), r=[B_mod, B_vecs], w=[B_gs])
        I(dve, lambda: nc.vector.tensor_copy(out=gs[:, 4, :, :], in_=modv[:, 24:32, :]), r=[B_mod], w=[B_gs])
        I(dve, lambda: nc.vector.tensor_tensor(out=gs[:, 5, :, :], in0=modv[:, 40:48, :], in1=ng(3), op=ALU.mult),
          r=[B_mod, B_vecs], w=[B_gs])

        for sn in ("ctx", "lat"):
            st = streams[sn]
            N, sidx = st["N"], st["sidx"]
            for ti in range(st["T"] // N):
                t0 = ti * N
                if from_input:
                    load_x(sn, ti, ctxT if sn == "ctx" else xT)
                    if add_pe and sn == "lat":
                        r0 = t0 // 64
                        nr = N // 64
                        for kt in range(4):
                            I(dve, lambda: nc.vector.tensor_tensor(
                                out=xt[:, kt, 0:N].rearrange("p (r c) -> p r c", c=64),
                                in0=xt[:, kt, 0:N].rearrange("p (r c) -> p r c", c=64),
                                in1=emb[:, kt, r0:r0 + nr].unsqueeze(2).to_broadcast([128, nr, 64]), op=ALU.add),
                              r=[B_xt, B_emb], w=[B_xt])
                            I(dve, lambda: nc.vector.tensor_tensor(
                                out=xt[:, 4 + kt, 0:N].rearrange("p (r c) -> p r c", c=64),
                                in0=xt[:, 4 + kt, 0:N].rearrange("p (r c) -> p r c", c=64),
                                in1=emb[:, kt, 0:64].unsqueeze(1).to_broadcast([128, nr, 64]), op=ALU.add),
                              r=[B_xt, B_emb], w=[B_xt])
                    store_x(sn, ti, co if sn == "ctx" else xo)
                else:
                    load_x(sn, ti, co if sn == "ctx" else xo)
                rms_mod(N, 0, 1, sidx)
                only_ssm = lastl and sn == "ctx"
                if not only_ssm:
                    wsg, Bwsg = wload_sc(par, GI_WIN + G_ZA_GATE, 4096)
                    wsa, Bwsa = wload_sc(par, GI_WIN + G_ZA_A, 4096)
                    ast, B_ast = b4[0], B_b4[0]
                    for j in range(4):
                        pg, Bpg = proj(wsg, Bwsg, j, KT, lambda kt: hb[:, kt, 0:N], [B_h], N)
                        gt, Bgt = ngate()
                        I(act, lambda: nc.scalar.activation(out=gt[:, 0:N], in_=pg[:, 0:N], func=AF.Sigmoid,
                                                            bias=vecs[:, V_BIN + 4 + j:V_BIN + 5 + j], scale=1.0),
                          r=[Bpg, B_vecs], w=[Bgt])
                        pa, Bpa = proj(wsa, Bwsa, j, KT, lambda kt: hb[:, kt, 0:N], [B_h], N)
                        I(dve, lambda: nc.vector.scalar_tensor_tensor(out=ast[:, j, 0:N], in0=pa[:, 0:N],
                                                                      scalar=vecs[:, V_BIN + j:V_BIN + j + 1], in1=gt[:, 0:N],
                                                                      op0=ALU.add, op1=ALU.mult),
                          r=[Bpa, B_vecs, Bgt], w=[B_ast])
                    a0 = st["aoff"] + t0
                    dma(sp, a_dr[:, :, a0:a0 + N], ast[:, :, 0:N], r=[B_ast], w=[B_ad[sn][ti]])
                wsz, Bwsz = wload_sc(par, GI_WIN + G_ZS, 4096)
                for j in range(4):
                    pz, Bpz = proj(wsz, Bwsz, j, KT, lambda kt: hb[:, kt, 0:N], [B_h], N)
                    z0 = st["off"] + t0
                    I(act, lambda: nc.scalar.activation(out=zs_all[:, j, z0:z0 + N], in_=pz[:, 0:N], func=AF.Identity,
                                                        bias=vecs[:, V_BIN + 16 + j:V_BIN + 17 + j], scale=1.0),
                      r=[Bpz, B_vecs], w=[B_zs[sn][ti]])

        phase_barrier(1)
        if li + 1 < len(layers):
            convert_layer(li + 1, (li + 1) % 2)
        for d in range(2):
            fwd = (d == 0)
            dma(sp, ssmp[:, :], ssmp_d[li, d, :, :], w=[B_ssmp])
            lam_re, lam_im, logdt = ssmp[:, 0:16], ssmp[:, 16:32], ssmp[:, 32:48]
            b_re = ssmp[:, 48:304].rearrange("p (a b) -> p a b", b=16)
            b_im = ssmp[:, 304:560].rearrange("p (a b) -> p a b", b=16)
            c_re = ssmp[:, 560:816].rearrange("p (a b) -> p a b", b=16)
            c_im = ssmp[:, 816:1072].rearrange("p (a b) -> p a b", b=16)
            DT, LR, ANG, DEN, FR, FI, T1, T2, PH, RHO = [sm[:, i, :] for i in range(10)]
            tau = cst[:, C_TAU:C_TAU + R + 1]
            I(act, lambda: nc.scalar.activation(out=DT, in_=logdt, func=AF.Exp), r=[B_ssmp], w=[B_sm])
            vv(LR, lam_re, DT, ALU.mult, [B_ssmp, B_sm], [B_sm])
            vv(ANG, lam_im, DT, ALU.mult, [B_ssmp, B_sm], [B_sm])

            def b17(x):
                return x.unsqueeze(2).to_broadcast([128, NP, R + 1])
            tau3 = tau.unsqueeze(1).to_broadcast([128, NP, R + 1])
            vv(t17[:, :, :], b17(LR), tau3, ALU.mult, [B_sm, B_cst], [B_t17])
            I(act, lambda: nc.scalar.activation(out=magt[:, :, :], in_=t17[:, :, :], func=AF.Exp), r=[B_t17], w=[B_magt])
            vv(t17[:, :, :], b17(ANG), tau3, ALU.mult, [B_sm, B_cst, B_magt], [B_t17])
            reduce_angle(t17[:, :, :], redi[:, 0:NP * (R + 1)].rearrange("p (a b) -> p a b", b=R + 1),
                         redf[:, 0:NP * (R + 1)].rearrange("p (a b) -> p a b", b=R + 1), [], B_t17, B_red)
            sincos(t17[:, :, :], sint[:, :, :], cost[:, :, :], B_t17, B_sint, B_cost)
            vv(ar[:, :, :], magt[:, :, :], cost[:, :, :], ALU.mult, [B_magt, B_cost], [B_apow])
            vv(ai[:, :, :], magt[:, :, :], sint[:, :, :], ALU.mult, [B_magt, B_sint], [B_apow])
            a_re, a_im = ar[:, :, 1], ai[:, :, 1]
            vv(DEN, lam_re, lam_re, ALU.mult, [B_ssmp], [B_sm])
            vv(T1, lam_im, lam_im, ALU.mult, [B_ssmp], [B_sm])
            vv(DEN, DEN, T1, ALU.add, [B_sm], [B_sm])
            I(dve, lambda: nc.vector.reciprocal(out=DEN, in_=DEN), r=[B_sm], w=[B_sm])
            I(dve, lambda: nc.vector.tensor_scalar(out=T1, in0=a_re, scalar1=-1.0, scalar2=None, op0=ALU.add),
              r=[B_apow, B_sm], w=[B_sm])
            vv(FR, T1, lam_re, ALU.mult, [B_sm, B_ssmp], [B_sm])
            vv(T2, a_im, lam_im, ALU.mult, [B_apow, B_ssmp, B_sm], [B_sm])
            vv(FR, FR, T2, ALU.add, [B_sm], [B_sm])
            vv(FR, FR, DEN, ALU.mult, [B_sm], [B_sm])
            vv(FI, a_im, lam_re, ALU.mult, [B_apow, B_ssmp, B_sm], [B_sm])
            vv(T2, T1, lam_im, ALU.mult, [B_sm, B_ssmp], [B_sm])
            vv(FI, FI, T2, ALU.subtract, [B_sm], [B_sm])
            vv(FI, FI, DEN, ALU.mult, [B_sm], [B_sm])
            I(dve, lambda: nc.vector.tensor_scalar(out=PH, in0=ANG, scalar1=float(R), scalar2=None, op0=ALU.mult),
              r=[B_sm], w=[B_sm])
            reduce_angle(PH, redi[:, 0:NP], redf[:, 0:NP], [], B_sm, B_red)
            I(dve, lambda: nc.vector.tensor_copy(out=RHO, in_=magt[:, :, R]), r=[B_magt, B_sm], w=[B_sm])

            def bq(x, nq):
                return x.unsqueeze(2).to_broadcast([128, nq, 16])

            def cmul(outr, Boutr, outi, Bouti, xr, xi, yr, yi, rB, nq):
                ta, tb = wt[0][:, 0:nq, :], wt[1][:, 0:nq, :]
                Ba, Bb = B_w[0], B_w[1]
                vv(ta, xr, yr, ALU.mult, rB, [Ba])
                vv(tb, xi, yi, ALU.mult, rB, [Bb])
                vv(outr, ta, tb, ALU.subtract, [Ba, Bb], [Boutr])
                vv(ta, xr, yi, ALU.mult, rB + [Boutr], [Ba])
                vv(tb, xi, yr, ALU.mult, rB + [Boutr], [Bb])
                vv(outi, ta, tb, ALU.add, [Ba, Bb], [Bouti])
            cmul(bbr[:, :, :], B_bb, bbi[:, :, :], B_bb, bq(FR, NP), bq(FI, NP), b_re, b_im, [B_sm, B_ssmp], NP)
            gpm = cst[:, C_GPM:C_GPM + 2]

            def masked(out, src, neg, rB, Bw, nq):
                for g_ in range(2):
                    I(dve, lambda: nc.vector.tensor_scalar(out=out[:, :, g_, :], in0=src, scalar1=gpm[:, g_:g_ + 1],
                                                           scalar2=(-1.0 if neg else 1.0), op0=ALU.mult, op1=ALU.mult),
                      r=rB + [B_cst], w=[Bw])
            masked(bbxr[:, :, :, :], bbr[:, :, :], False, [B_bb], B_bbx, NP)
            masked(bbxi[:, :, :, :], bbi[:, :, :], False, [B_bb], B_bbx, NP)
            w3, w4 = wt[2][:, 0:4, :], wt[3][:, 0:4, :]

            for kt in range(4):
                sl = slice(4 * kt, 4 * kt + 4)
                ta4 = review("SH", [R + 1, 4, 16], F32); tb4 = review("TK", [R + 1, 4, 16], F32)
                Wr4 = review("Marr", [R + 1, 4, 16], F32); Wi4 = review("Garr", [R + 1, 4, 16], F32)

                def AP4(base, q0, e0, dims):
                    ps_ = base.ap[0][0]
                    Bn = base.ap[1][0]
                    return bass.AP(tensor=base.tensor, offset=base.offset + q0 * Bn + e0, ap=[[ps_, 128]] + [list(d_) for d_ in dims])

                def cm4(nt, xr, xi, yr, yi, rB):
                    vv(ta4[:, 0:nt], xr, yr, ALU.mult, rB, [B_SH])
                    vv(tb4[:, 0:nt], xi, yi, ALU.mult, rB, [B_TK])
                    vv(Wr4[:, 0:nt], ta4[:, 0:nt], tb4[:, 0:nt], ALU.subtract, [B_SH, B_TK], [B_M])
                    vv(ta4[:, 0:nt], xr, yi, ALU.mult, rB + [B_M], [B_SH])
                    vv(tb4[:, 0:nt], xi, yr, ALU.mult, rB + [B_M], [B_TK])
                    vv(Wi4[:, 0:nt], ta4[:, 0:nt], tb4[:, 0:nt], ALU.add, [B_SH, B_TK], [B_G])

                def masked4(nt, t_lo, src, Bsrc, dst, Bdst, neg):
                    for g_ in range(2):
                        I(dve, lambda: nc.vector.tensor_scalar(out=dst[:, t_lo:t_lo + nt, :, g_, :], in0=src[:, t_lo:t_lo + nt],
                                                               scalar1=gpm[:, g_:g_ + 1], scalar2=(-1.0 if neg else 1.0),
                                                               op0=ALU.mult, op1=ALU.mult), r=[Bsrc, B_cst], w=[Bdst])
                e0 = (R - 1) if fwd else 0
                dtau = -1 if fwd else 1
                cm4(R, AP4(ar, 4 * kt, e0, [[dtau, R], [R + 1, 4], [0, 16]]), AP4(ai, 4 * kt, e0, [[dtau, R], [R + 1, 4], [0, 16]]),
                    AP4(bbr, 4 * kt, 0, [[0, R], [16, 4], [1, 16]]), AP4(bbi, 4 * kt, 0, [[0, R], [16, 4], [1, 16]]), [B_apow, B_bb])
                masked4(R, 0, Wr4, B_M, xmall[0], B_xmall[0], False)
                masked4(R, 0, Wi4, B_G, xmall[1], B_xmall[1], False)
                for s0 in range(0, R, 4):
                    pb_, Bpb = npsb()
                    for sl_ in range(4):
                        for reim in range(2):
                            I(pe, lambda: nc.tensor.transpose(out=pb_[:, (sl_ * 2 + reim) * 128:(sl_ * 2 + reim + 1) * 128],
                                                              in_=xmall[reim][:, s0 + sl_, :, :, :].rearrange("p a b c -> p (a b c)"),
                                                              identity=identb[:, :]), r=[B_xmall[reim], B_cst], w=[Bpb], rawself=False)
                    I(act, lambda: nc.scalar.copy(out=Pk[:, s0:s0 + 4, :, :].rearrange("p s k m -> p (s k m)"), in_=pb_[:, 0:1024]),
                      r=[Bpb], w=[B_P])
                for sn in ("ctx", "lat"):
                    st = streams[sn]
                    Nc = st["T"] // R
                    zoff = st["off"]
                    hbase = 0 if sn == "ctx" else (NCC + 1)
                    I(dve, lambda: nc.vector.tensor_tensor(out=angk[:, :, 0:Nc + 1],
                                                           in0=PH[:, sl].unsqueeze(2).to_broadcast([128, 4, Nc + 1]),
                                                           in1=cst[:, C_KIDX:C_KIDX + Nc + 1].unsqueeze(1).to_broadcast([128, 4, Nc + 1]),
                                                           op=ALU.mult), r=[B_sm, B_cst], w=[B_angk])
                    reduce_angle(angk[:, :, 0:Nc + 1], redi3[:, :, 0:Nc + 1], redf3[:, :, 0:Nc + 1], [], B_angk, B_red)
                    sincos(angk[:, :, 0:Nc + 1], sink[:, :, 0:Nc + 1], cosk[:, :, 0:Nc + 1], B_angk, B_ck, B_ck)
                    for q in range(4):
                        for reim in range(2):
                            p, Bp = nps()
                            for s_ in range(R):
                                rhs = V(zs_all, 32 * q, 32, kt * TALL + zoff + s_, [[R, Nc]])
                                mm(p[:, 0:Nc], Pk[32 * q:32 * q + 32, s_, reim, :], rhs, s_ == 0, s_ == R - 1,
                                   r=[B_P] + B_zs[sn], w=[Bp], tp=(32 * q, 0))
                            if fwd:
                                dst = SH[:, q, reim, 0:Nc]
                            else:
                                dst = SH[:, q, reim, 0:Nc][:, ::-1]
                            I(act, lambda: nc.scalar.copy(out=dst, in_=p[:, 0:Nc]), r=[Bp], w=[B_SH])
                    ck = cosk[:, :, 1:Nc + 1]
                    sk = sink[:, :, 1:Nc + 1]
                    Sr, Si = SH[:, :, 0, 0:Nc], SH[:, :, 1, 0:Nc]
                    t1r, t1i = TK[:, :, 0, 0:Nc], TK[:, :, 1, 0:Nc]
                    vv(t1r, Sr, ck, ALU.mult, [B_SH, B_ck], [B_TK])
                    vv(t1i, Si, sk, ALU.mult, [B_SH, B_ck], [B_TK])
                    vv(Marr[:, :, 0, 0:Nc], t1r, t1i, ALU.add, [B_TK], [B_M])
                    vv(t1r, Si, ck, ALU.mult, [B_SH, B_ck, B_M], [B_TK])
                    vv(t1i, Sr, sk, ALU.mult, [B_SH, B_ck, B_M], [B_TK])
                    vv(Marr[:, :, 1, 0:Nc], t1r, t1i, ALU.subtract, [B_TK], [B_M])
                    if sn == "ctx":
                        I(dve, lambda: nc.vector.memset(Garr[:, :, :, 0], 0.0), w=[B_G])
                    else:
                        I(dve, lambda: nc.vector.tensor_copy(out=Garr[:, :, :, 0], in_=hfin[:, :, :]), r=[B_hfin], w=[B_G])
                    for q in range(4):
                        for reim in range(2):
                            I(dve, lambda: nc.vector.tensor_tensor_scan(
                                out=Garr[:, q, reim, 1:Nc + 1], data0=RHO[:, 4 * kt + q:4 * kt + q + 1].to_broadcast([128, Nc]),
                                data1=Marr[:, q, reim, 0:Nc], initial=Garr[:, q, reim, 0:1], op0=ALU.mult, op1=ALU.add),
                              r=[B_M, B_sm, B_G], w=[B_G])
                    cj, sj = cosk[:, :, 0:Nc + 1], sink[:, :, 0:Nc + 1]
                    Gr, Gi = Garr[:, :, 0, 0:Nc + 1], Garr[:, :, 1, 0:Nc + 1]
                    u1, u2 = TK[:, :, 0, 0:Nc + 1], TK[:, :, 1, 0:Nc + 1]
                    vv(u1, Gr, cj, ALU.mult, [B_G, B_ck], [B_TK])
                    vv(u2, Gi, sj, ALU.mult, [B_G, B_ck], [B_TK])
                    vv(SH[:, :, 0, 0:Nc + 1], u1, u2, ALU.subtract, [B_TK], [B_SH])
                    vv(u1, Gi, cj, ALU.mult, [B_G, B_ck, B_SH], [B_TK])
                    vv(u2, Gr, sj, ALU.mult, [B_G, B_ck, B_SH], [B_TK])
                    vv(SH[:, :, 1, 0:Nc + 1], u1, u2, ALU.add, [B_TK], [B_SH])
                    I(act, lambda: nc.scalar.copy(out=Hk[:, :, :, hbase:hbase + Nc + 1], in_=SH[:, :, :, 0:Nc + 1]),
                      r=[B_SH], w=[B_H])
                    if sn == "ctx":
                        I(dve, lambda: nc.vector.tensor_copy(out=hfin[:, :, :], in_=SH[:, :, :, Nc]), r=[B_SH], w=[B_hfin])
                c_re3 = c_re
                c_im3 = c_im
                cm4(R + 1, AP4(c_re3, 4 * kt, 0, [[0, R + 1], [16, 4], [1, 16]]), AP4(c_im3, 4 * kt, 0, [[0, R + 1], [16, 4], [1, 16]]),
                    AP4(ar, 4 * kt, 0, [[1, R + 1], [R + 1, 4], [0, 16]]), AP4(ai, 4 * kt, 0, [[1, R + 1], [R + 1, 4], [0, 16]]),
                    [B_apow, B_ssmp])
                masked4(R + 1, 0, Wr4, B_M, xmall[0], B_xmall[0], False)
                masked4(R + 1, 0, Wi4, B_G, xmall[1], B_xmall[1], True)
                for t0_ in range(0, R, 4):
                    p, Bp = nps()
                    mm(p[:, 0:512], bbxr[:, sl, :, :].rearrange("p a b c -> p (a b c)"),
                       xmall[0][:, t0_:t0_ + 4, :, :, :].rearrange("p t a b c -> p (t a b c)"), True, False, r=[B_bbx, B_xmall[0]], w=[Bp])
                    mm(p[:, 0:512], bbxi[:, sl, :, :].rearrange("p a b c -> p (a b c)"),
                       xmall[1][:, t0_:t0_ + 4, :, :, :].rearrange("p t a b c -> p (t a b c)"), False, True, r=[B_bbx, B_xmall[1]], w=[Bp])
                    I(dve, lambda: nc.vector.tensor_tensor(out=Kk[:, t0_:t0_ + 4, :], in0=p[:, 0:512].rearrange("p (t m) -> p t m", t=4),
                                                           in1=kmask.unsqueeze(1).to_broadcast([128, 4, 128]), op=ALU.mult),
                      r=[Bp, B_cst], w=[B_K])
                psq = Qk.ap[0][0]
                for reim in range(2):
                    s_start = 0 if fwd else (R - 1)
                    s_step = 32 if fwd else -32
                    qout = bass.AP(tensor=Qk.tensor, offset=Qk.offset + reim * R * 32 + s_start * 32,
                                   ap=[[psq, 128], [s_step, R], [2 * R * 32, 4], [1, 32]])
                    I(dve, lambda: nc.vector.tensor_copy(out=qout, in_=xmall[reim][:, 1:R + 1, :, :, :].rearrange("p t q g j -> p t q (g j)")),
                      r=[B_xmall[reim]], w=[B_Q])
                for sn in ("ctx", "lat"):
                    if sn == "ctx" and lastl:
                        continue
                    st = streams[sn]
                    Nc = st["T"] // R
                    zoff = st["off"]
                    hbase = 0 if sn == "ctx" else (NCC + 1)
                    N = st["N"]
                    cpt = N // R
                    for ti in range(st["T"] // N):
                        t0 = ti * N
                        c0 = ti * cpt
                        cb = cpt
                        pint, Bpint = nps()
                        for tv in range(R):
                            n_s = R - tv
                            if fwd:
                                rhs = V(zs_all, 0, 128, kt * TALL + zoff + t0, [[R, cpt], [1, n_s]])
                                out = V(pint, 0, 128, tv, [[R, cpt], [1, n_s]])
                            else:
                                rhs = V(zs_all, 0, 128, kt * TALL + zoff + t0 + tv, [[R, cpt], [1, n_s]])
                                out = V(pint, 0, 128, 0, [[R, cpt], [1, n_s]])
                            mm(out, Kk[:, tv, :], rhs, tv == 0, tv == R - 1, r=[B_K, B_zs[sn][ti]], w=[Bpint])
                        k0 = c0 if fwd else (Nc - c0 - cb)
                        for q in range(4):
                            p, Bp = nps()
                            mm(p[0:cb, 0:512], Hk[:, q, 0, hbase + k0:hbase + k0 + cb], Qk[:, q, 0, :, :].rearrange("p a b -> p (a b)"),
                               True, False, r=[B_H, B_Q], w=[Bp])
                            mm(p[0:cb, 0:512], Hk[:, q, 1, hbase + k0:hbase + k0 + cb], Qk[:, q, 1, :, :].rearrange("p a b -> p (a b)"),
                               False, True, r=[B_H, B_Q], w=[Bp])
                            I(act, lambda: nc.scalar.copy(out=YI[0:cb, :, q, :], in_=p[0:cb, 0:512].rearrange("p (s m) -> p s m", m=32)),
                              r=[Bp], w=[B_YI])
                        for sg_ in range(2):
                            pb_, Bpb = npsb()
                            for s8 in range(8):
                                s_ = sg_ * 8 + s8
                                I(pe, lambda: nc.tensor.transpose(
                                    out=pb_[:, s8 * 128:s8 * 128 + cb],
                                    in_=YI[0:cb, s_, :, :].rearrange("p a b -> p (a b)"),
                                    identity=identb[0:cb, 0:cb]), r=[B_YI, B_cst], w=[Bpb], rawself=False)
                            if fwd:
                                src = V(pb_, 0, 128, 0, [[1, cb], [128, 8]])
                            else:
                                src = V(pb_, 0, 128, cb - 1, [[-1, cb], [128, 8]])
                            dst = ystage[:, 0:cb * R].rearrange("p (c s) -> p c s", s=R)[:, :, sg_ * 8:sg_ * 8 + 8]
                            I(dve, lambda: nc.vector.tensor_copy(out=dst, in_=src), r=[Bpb], w=[B_yst])
                        yv = y_all[:, kt, zoff + t0:zoff + t0 + N]
                        if fwd:
                            vv(yv, pint[:, 0:N], ystage[:, 0:N], ALU.add, [Bpint, B_yst], [B_y[sn][ti]])
                        else:
                            vv(ystage[:, 0:N], pint[:, 0:N], ystage[:, 0:N], ALU.add, [Bpint, B_yst], [B_yst])
                            vv(yv, yv, ystage[:, 0:N], ALU.add, [B_yst, B_y[sn][ti]], [B_y[sn][ti]])

        phase_barrier(0)
        for sn in ("ctx", "lat"):
            if sn == "ctx" and lastl:
                continue
            st = streams[sn]
            N, sidx = st["N"], st["sidx"]
            ntile = st["T"] // N
            for ti in range(ntile):
                t0 = ti * N
                z0 = st["off"] + t0
                a0 = st["aoff"] + t0
                load_x(sn, ti, co if sn == "ctx" else xo)
                nb = [B_ad[sn][ti]] + ([B_ad[sn][ti - 1]] if ti > 0 else []) + ([B_ad[sn][ti + 1]] if ti + 1 < ntile else []) + [B_apad]
                dma(sp, a_ld[:, :, 0:N + 30], a_dr[:, :, a0 - 15:a0 + N + 15], r=nb, w=[B_ald])
                yc = mergedb[:, 0:4, :]
                y2 = mergedb[:, 4:8, :]
                for kt in range(4):
                    t, Bt = ntmp()
                    I(dve, lambda: nc.vector.scalar_tensor_tensor(out=t[:, 0:N], in0=zs_all[:, kt, z0:z0 + N],
                                                                  scalar=vecs[:, V_SSMD + kt:V_SSMD + kt + 1],
                                                                  in1=y_all[:, kt, z0:z0 + N], op0=ALU.mult, op1=ALU.add),
                      r=[B_zs[sn][ti], B_y[sn][ti], B_vecs], w=[Bt])
                    I(act, lambda: nc.scalar.activation(out=yc[:, kt, 0:N], in_=t[:, 0:N], func=AF.Gelu_apprx_tanh),
                      r=[Bt], w=[B_mergedb])
                rms_mod(N, 0, 1, sidx)
                hr = lambda kt: hb[:, kt, 0:N]
                ub, vn, sgo = b4[0], b4[1], b4[2]
                B_ub, B_vn, B_sgo = B_b4
                wsu, Bwsu = wload_sc(par, GI_WIN + G_ZB_U, 4096)
                wsv, Bwsv = wload_sc(par, GI_WIN + G_ZB_V, 4096)
                for j in range(4):
                    p, Bp = proj(wsu, Bwsu, j, KT, hr, [B_h], N)
                    I(act, lambda: nc.scalar.activation(out=ub[:, j, 0:N], in_=p[:, 0:N], func=AF.Gelu_apprx_tanh,
                                                        bias=vecs[:, V_BIN + 8 + j:V_BIN + 9 + j], scale=1.0), r=[Bp, B_vecs], w=[B_ub])
                    p, Bp = proj(wsv, Bwsv, j, KT, hr, [B_h], N)
                    I(act, lambda: nc.scalar.activation(out=tmp4[:, j, 0:N], in_=p[:, 0:N], func=AF.Gelu_apprx_tanh,
                                                        bias=vecs[:, V_BIN + 12 + j:V_BIN + 13 + j], scale=1.0), r=[Bp, B_vecs], w=[B_tmp4])

                ln_state = {}

                def ln4_stats():
                    I(act, lambda: nc.scalar.activation(out=sqb[:, 0:4, 0:N], in_=tmp4[:, :, 0:N], func=AF.Square), r=[B_tmp4], w=[B_sqb])
                    I(act, lambda: nc.scalar.copy(out=sqb[:, 4:8, 0:N], in_=tmp4[:, :, 0:N]), r=[B_tmp4], w=[B_sqb])
                    pm_, Bpm_ = nps()
                    pq_, Bpq_ = nps()
                    for kt in range(4):
                        mm(pm_[:, 0:N], ones5[:, :], sqb[:, 4 + kt, 0:N], kt == 0, kt == 3, r=[B_sqb, B_cst], w=[Bpm_])
                    for kt in range(4):
                        mm(pq_[:, 0:N], ones5[:, :], sqb[:, kt, 0:N], kt == 0, kt == 3, r=[B_sqb, B_cst], w=[Bpq_])
                    mean, Bmean = ntmp()
                    I(act, lambda: nc.scalar.copy(out=mean[:, 0:N], in_=pm_[:, 0:N]), r=[Bpm_], w=[Bmean])
                    var, Bvar = ntmp()
                    vv(var[:, 0:N], mean[:, 0:N], mean[:, 0:N], ALU.mult, [Bmean], [Bvar])
                    vv(var[:, 0:N], pq_[:, 0:N], var[:, 0:N], ALU.subtract, [Bpq_, Bvar], [Bvar])
                    I(act, lambda: nc.scalar.activation(out=rstd[:, 0:N], in_=var[:, 0:N], func=AF.Sqrt, bias=epsb[:, 0:1], scale=1.0),
                      r=[Bvar, B_cst], w=[B_rstd])
                    I(dve, lambda: nc.vector.reciprocal(out=rstd[:, 0:N], in_=rstd[:, 0:N]), r=[B_rstd], w=[B_rstd])
                    I(dve, lambda: nc.vector.tensor_copy(out=var[:, 0:N], in_=mean[:, 0:N]), r=[Bmean], w=[Bvar])
                    ln_state["mean"] = (var, Bvar)

                def ln4_norm(kt, gcol, bcol, dst, Bdst, func):
                    mean, Bmean = ln_state["mean"]
                    vv(tmp4[:, kt, 0:N], tmp4[:, kt, 0:N], mean[:, 0:N], ALU.subtract, [B_tmp4, Bmean], [B_tmp4])
                    vv(tmp4[:, kt, 0:N], tmp4[:, kt, 0:N], rstd[:, 0:N], ALU.mult, [B_tmp4, B_rstd], [B_tmp4])
                    I(act, lambda: nc.scalar.activation(out=dst[:, kt, 0:N], in_=tmp4[:, kt, 0:N], func=func,
                                                        bias=vecs[:, bcol + kt:bcol + kt + 1], scale=vecs[:, gcol + kt:gcol + kt + 1]),
                      r=[B_tmp4, B_vecs], w=[Bdst])

                def layernorm4(gcol, bcol, dst, Bdst, func):
                    ln4_stats()
                    for kt in range(4):
                        ln4_norm(kt, gcol, bcol, dst, Bdst, func)
                def branch_out(wd, src, Bsrc, nk, g0, bcolout, mode, between=None):
                    wsb, Bwsb = wload_sc(par, wd, 4096)
                    for gg in range(2):
                        wsg_, Bwsg_ = wload_sc(par, GI_WIN + g0 + gg, 4096)
                        for jj in range(4):
                            mt = gg * 4 + jj
                            pg, Bpg = proj(wsg_, Bwsg_, jj, KT, hr, [B_h], N)
                            gt, Bgt = ngate()
                            bi = WIN_ORDER[(g0 + gg) * 4 + jj]
                            I(act, lambda: nc.scalar.activation(out=gt[:, 0:N], in_=pg[:, 0:N], func=AF.Sigmoid,
                                                                bias=vecs[:, V_BIN + bi:V_BIN + bi + 1], scale=1.0),
                              r=[Bpg, B_vecs], w=[Bgt])
                            po, Bpo = proj(wsb, Bwsb, mt, nk, lambda kt: src[:, kt, 0:N], [Bsrc], N)
                            if mode == 0:
                                I(dve, lambda: nc.vector.scalar_tensor_tensor(out=big1[:, mt, 0:N], in0=po[:, 0:N],
                                                                              scalar=vecs[:, bcolout + mt:bcolout + mt + 1],
                                                                              in1=gt[:, 0:N], op0=ALU.add, op1=ALU.mult),
                                  r=[Bpo, B_vecs, Bgt], w=[B_big1])
                            else:
                                t, Bt = ntmp()
                                I(dve, lambda: nc.vector.scalar_tensor_tensor(out=t[:, 0:N], in0=po[:, 0:N],
                                                                              scalar=vecs[:, bcolout + mt:bcolout + mt + 1],
                                                                              in1=gt[:, 0:N], op0=ALU.add, op1=ALU.mult),
                                  r=[Bpo, B_vecs, Bgt], w=[Bt])
                                if mode == 1:
                                    vv(big1[:, mt, 0:N], big1[:, mt, 0:N], t[:, 0:N], ALU.add, [B_big1, Bt], [B_big1])
                                else:
                                    vv(mergedb[:, mt, 0:N], big1[:, mt, 0:N], t[:, 0:N], ALU.add, [B_big1, Bt], [B_mergedb])
                            if between is not None:
                                between(mt)
                ln4_stats()
                wsq, Bwsq = wload_sc(par, GI_WGLU, 2048)
                for j in range(4):
                    p, Bp = proj(wsq, Bwsq, j, 4, lambda kt: yc[:, kt, 0:N], [B_mergedb], N)
                    gt, Bgt = ngate()
                    I(act, lambda: nc.scalar.activation(out=gt[:, 0:N], in_=p[:, 0:N], func=AF.Sigmoid,
                                                        bias=vecs[:, V_BGLU + j:V_BGLU + j + 1], scale=1.0), r=[Bp, B_vecs], w=[Bgt])
                    vv(y2[:, j, 0:N], yc[:, j, 0:N], gt[:, 0:N], ALU.mult, [B_mergedb, Bgt], [B_mergedb])

                def _between_c(mt):
                    if mt % 2 == 1:
                        ln4_norm(mt // 2, V_SLNG, V_SLNB, vn, B_vn, AF.Identity)
                branch_out(GI_WSS, y2, B_mergedb, 4, G_ZG_C, V_SSBO, 0, between=_between_c)
                nch = N // 128
                for kt in range(4):
                    p, Bp = nps()
                    pb_, Bpb = npsb()
                    for cb_ in range(nch):
                        I(pe, lambda: nc.tensor.transpose(out=pb_[:, cb_ * 128:(cb_ + 1) * 128], in_=vn[:, kt, cb_ * 128:(cb_ + 1) * 128],
                                                          identity=identb[:, :]), r=[B_vn, B_cst], w=[Bpb], rawself=False)
                    I(act, lambda: nc.scalar.copy(out=vT[:, 0:nch, :].rearrange("p c m -> p (c m)"), in_=pb_[:, 0:nch * 128]),
                      r=[Bpb], w=[B_vT])
                    for cb_ in range(nch):
                        for hh in range(2):
                            mm(p[64 * hh:64 * hh + 64, cb_ * 128:(cb_ + 1) * 128], vT[:, cb_, 64 * hh:64 * hh + 64],
                               swT[:, 2 * kt + hh, :], True, True, r=[B_vT, B_swT], w=[Bp], tp=(0, 64 * hh))
                    t, Bt = ntmp()
                    I(dve, lambda: nc.vector.tensor_tensor(out=t[:, 0:N].rearrange("p (c q) -> p c q", q=128),
                                                           in0=p[:, 0:N].rearrange("p (c q) -> p c q", q=128),
                                                           in1=sgb[:, kt, :].unsqueeze(1).to_broadcast([128, nch, 128]), op=ALU.add),
                      r=[Bp, B_sgb], w=[Bt])
                    vv(sgo[:, kt, 0:N], t[:, 0:N], ub[:, kt, 0:N], ALU.mult, [Bt, B_ub], [B_sgo])

                branch_out(GI_WSO, sgo, B_sgo, 4, G_ZG_B, V_SBO, 1)
                ac = b4[0]; B_ac = B_b4[0]
                for kt in range(4):
                    dg, Bdg = (diag, B_diag) if kt % 2 == 0 else (diag2, B_diag2)
                    for kk in range(31):
                        I(dve, lambda: nc.vector.tensor_scalar(out=dg[:, kk, :], in0=identb[:, :], scalar1=cw[:, kt, kk:kk + 1],
                                                               scalar2=None, op0=ALU.mult), r=[B_cst, B_cw], w=[Bdg])
                    p, Bp = nps()
                    for kk in range(31):
                        mm(p[:, 0:N], dg[:, kk, :], a_ld[:, kt, kk:kk + N], kk == 0, kk == 30, r=[Bdg, B_ald], w=[Bp])
                    I(act, lambda: nc.scalar.activation(out=tmp4[:, kt, 0:N], in_=p[:, 0:N], func=AF.Identity,
                                                        bias=vecs[:, V_CONVB + kt:V_CONVB + kt + 1], scale=1.0), r=[Bp, B_vecs], w=[B_tmp4])
                layernorm4(V_CLNG, V_CLNB, ac, B_ac, AF.Silu)
                branch_out(GI_WCO, ac, B_ac, 4, G_ZG_A, V_CBO, 2)
                for gg in range(2):
                    wsw, Bwsw = wload_sc(par, GI_WO + gg, 4096)
                    for jj in range(4):
                        mt = gg * 4 + jj
                        p, Bp = proj(wsw, Bwsw, jj, KT, lambda kt: mergedb[:, kt, 0:N], [B_mergedb], N)
                        I(act, lambda: nc.scalar.activation(out=big1[:, mt, 0:N], in_=p[:, 0:N], func=AF.Identity,
                                                            bias=vecs[:, V_BO + mt:V_BO + mt + 1], scale=1.0), r=[Bp, B_vecs], w=[B_big1])
                resid_update(big1, B_big1, N, 2, sidx)
                rms_mod(N, 3, 4, sidx)
                for g in range(11):
                    wsf, Bwsf = wload_sc(par, GI_WUP + g, 4096)
                    for jj in range(2):
                        j = g * 2 + jj
                        pg, Bpg = proj(wsf, Bwsf, 2 * jj, KT, hr, [B_h], N)
                        pu, Bpu = proj(wsf, Bwsf, 2 * jj + 1, KT, hr, [B_h], N)
                        t, Bt = ntmp()
                        I(act, lambda: nc.scalar.activation(out=t[:, 0:N], in_=pg[:, 0:N], func=AF.Silu), r=[Bpg], w=[Bt])
                        vv(actb[:, j, 0:N], t[:, 0:N], pu[:, 0:N], ALU.mult, [Bt, Bpu], [B_act])
                for mt in range(8):
                    wsd, Bwsd = wload_sc(par, GI_WDN + mt, 2816)
                    p, Bp = proj(wsd, Bwsd, 0, FT, lambda kt: actb[:, kt, 0:N], [B_act], N)
                    I(act, lambda: nc.scalar.copy(out=big1[:, mt, 0:N], in_=p[:, 0:N]), r=[Bp], w=[B_big1])
                resid_update(big1, B_big1, N, 5, sidx)
                store_x(sn, ti, co if sn == "ctx" else xo)

    k.final_wait(sp, [b for bl in B_xd.values() for b in bl])
    print("instructions emitted:", k.nins, "per-engine", k.ncnt, "waits", k.nwait, "dmas", {q.name: q.ndma for q in (k.pool, k.sp)})
    return nc


def tile_w(w, group):
    Kd, M = w.shape
    kt = Kd // 128
    G = M // 128 // group
    w5 = w.reshape(kt, 128, G, group, 128)
    return np.ascontiguousarray(w5.transpose(2, 1, 3, 0, 4)).reshape(G, 128, group * kt * 128)


def fm(v, nt):
    return np.ascontiguousarray(v.reshape(nt, 128).T)


def make_consts():
    c = np.zeros((128, NCST), np.float32)
    c[:, C_ID:C_ID + 128] = np.eye(128, dtype=np.float32)
    blk = np.arange(128) // 16
    c[:, C_KM:C_KM + 128] = (blk[:, None] == blk[None, :]).astype(np.float32)
    gp = np.arange(128) // 64
    c[:, C_GPM + 0] = (gp == 0)
    c[:, C_GPM + 1] = (gp == 1)
    c[:, C_TAU:C_TAU + 17] = np.arange(17, dtype=np.float32)[None, :]
    c[:, C_KIDX:C_KIDX + NKIDX] = np.arange(NKIDX, dtype=np.float32)[None, :]
    c[:, C_POS:C_POS + 64] = np.arange(64, dtype=np.float32)[None, :]
    half = 256
    for kt in range(4):
        dd = kt * 128 + np.arange(128)
        j = dd % half
        om = (1.0 / (10000.0 ** (j.astype(np.float32) / np.float32(half)))).astype(np.float32)
        c[:, C_OM + kt] = om
        c[:, C_PH + kt] = np.where(dd >= half, np.float32(math.pi / 2), np.float32(0.0))
    return c


def prep_shared(inp):
    L = DEPTH
    f = lambda a: np.asarray(a, dtype=np.float32)
    sh = {}
    vecs = np.zeros((L, 128, NV), np.float32)
    for l in range(L):
        vecs[l, :, V_BIN:V_BIN + 44] = fm(f(inp["b_in"][l]), 44)
        vecs[l, :, V_CONVB:V_CONVB + 4] = fm(f(inp["conv_b"][l]), 4)
        vecs[l, :, V_CLNG:V_CLNG + 4] = fm(f(inp["conv_ln_g"][l]), 4)
        vecs[l, :, V_CLNB:V_CLNB + 4] = fm(f(inp["conv_ln_b"][l]), 4)
        vecs[l, :, V_SLNG:V_SLNG + 4] = fm(f(inp["sgu_ln_g"][l]), 4)
        vecs[l, :, V_SLNB:V_SLNB + 4] = fm(f(inp["sgu_ln_b"][l]), 4)
        vecs[l, :, V_SSMD:V_SSMD + 4] = fm(f(inp["ssm_d"][l]), 4)
        vecs[l, :, V_BGLU:V_BGLU + 4] = fm(f(inp["ssm_b_glu"][l]), 4)
        vecs[l, :, V_CBO:V_CBO + 8] = fm(f(inp["conv_b_out"][l]), 8)
        vecs[l, :, V_SBO:V_SBO + 8] = fm(f(inp["sgu_b_out"][l]), 8)
        vecs[l, :, V_SSBO:V_SSBO + 8] = fm(f(inp["ssm_b_out"][l]), 8)
        vecs[l, :, V_BO:V_BO + 8] = fm(f(inp["b_o"][l]), 8)
        for i in range(4):
            vecs[l, :, V_NG + 8 * i:V_NG + 8 * i + 8] = fm(f(inp["norm_g"][l, i]), 8)
        vecs[l, :, V_BADA:V_BADA + 48] = fm(f(inp["b_ada"][l]), 48)
    sh["vecs"] = vecs
    ssmp = np.zeros((L, 2, 128, 1072), np.float32)

    def sm_gp(a):
        return np.ascontiguousarray(a.reshape(16, 2, 64).transpose(1, 2, 0)).reshape(128, 16)

    def sm_gpi(a):
        return np.ascontiguousarray(a.reshape(16, 2, 64, 16).transpose(1, 2, 0, 3)).reshape(128, 256)
    for l in range(L):
        for d in range(2):
            ssmp[l, d, :, 0:16] = sm_gp(f(inp["ssm_lam_re"][l, d]))
            ssmp[l, d, :, 16:32] = sm_gp(f(inp["ssm_lam_im"][l, d]))
            ssmp[l, d, :, 32:48] = sm_gp(np.repeat(f(inp["ssm_log_dt"][l, d])[:, None], 64, axis=1))
            ssmp[l, d, :, 48:304] = sm_gpi(f(inp["ssm_b_re"][l, d]))
            ssmp[l, d, :, 304:560] = sm_gpi(f(inp["ssm_b_im"][l, d]))
            ssmp[l, d, :, 560:816] = sm_gpi(f(inp["ssm_c_re"][l, d]).transpose(0, 2, 1))
            ssmp[l, d, :, 816:1072] = sm_gpi(f(inp["ssm_c_im"][l, d]).transpose(0, 2, 1))
    sh["ssmp"] = ssmp
    sgu_w = f(inp["sgu_w"])
    sh["swT"] = np.ascontiguousarray(sgu_w.transpose(0, 3, 1, 2)).reshape(L, 128, 1024)
    sgu_b = f(inp["sgu_b"])
    sgb = np.zeros((L, 128, 4, 128), np.float32)
    for kt in range(4):
        sgb[:, 0:64, kt, :] = sgu_b[:, 2 * kt, None, :]
        sgb[:, 64:128, kt, :] = sgu_b[:, 2 * kt + 1, None, :]
    sh["sgb"] = sgb.reshape(L, 128, 512)
    conv_w = f(inp["conv_w"])
    sh["cw"] = np.ascontiguousarray(conv_w.reshape(L, 31, 4, 128).transpose(0, 3, 2, 1)).reshape(L, 128, 124)
    sh["wada"] = np.stack([tile_w(f(inp["w_ada"][l]), 4) for l in range(L)])
    perm = np.concatenate([np.arange(m * 128, (m + 1) * 128) for m in WIN_ORDER])
    sh["win"] = np.stack([tile_w(f(inp["w_in"][l])[:, perm], 4) for l in range(L)])
    sh["wco"] = np.stack([tile_w(f(inp["conv_w_out"][l]), 8)[0] for l in range(L)])
    sh["wso"] = np.stack([tile_w(f(inp["sgu_w_out"][l]), 8)[0] for l in range(L)])
    sh["wss"] = np.stack([tile_w(f(inp["ssm_w_out"][l]), 8)[0] for l in range(L)])
    sh["wglu"] = np.stack([tile_w(f(inp["ssm_w_glu"][l]), 4)[0] for l in range(L)])
    sh["wo"] = np.stack([tile_w(f(inp["w_o"][l]), 4) for l in range(L)])
    upo = []
    for j in range(FT):
        upo += [j, FT + j]
    permu = np.concatenate([np.arange(m * 128, (m + 1) * 128) for m in upo])
    sh["wup"] = np.stack([tile_w(f(inp["ffn_w_up"][l])[:, permu], 4) for l in range(L)])
    sh["wdn"] = np.stack([tile_w(f(inp["ffn_w_down"][l]), 1) for l in range(L)])
    sh["cst"] = make_consts()
    return sh


_NC_CACHE = {}


def kernel(**inputs):
    x = np.asarray(inputs["x"], np.float32)
    c = np.asarray(inputs["c"], np.float32)
    ctx = np.asarray(inputs["ctx"], np.float32)
    c_ctx = np.asarray(inputs["c_ctx"], np.float32)
    sh = prep_shared(inputs)
    ncores = 8
    per_core = []
    for core in range(ncores):
        b = core % 4
        m = dict(sh)
        m["xT"] = np.ascontiguousarray(x[b].T).reshape(KT, 128, T_LAT)
        m["ctxT"] = np.ascontiguousarray(ctx[b].T).reshape(KT, 128, T_CTX)
        sc = np.zeros((128, KT, 2), np.float32)
        sc[:, :, 0] = fm(c[b], KT)
        sc[:, :, 1] = fm(c_ctx, KT)
        m["sc_in"] = sc.reshape(128, KT * 2)
        per_core.append(m)
    if FUSED:
        plan = [(list(range(DEPTH)), True)]
    else:
        plan = [([l], l == 0) for l in range(DEPTH)]
    res = None
    for (layers, first) in plan:
        key = (len(layers), first, layers[-1] == DEPTH - 1, layers[0] if len(layers) > 1 else -1)
        if key not in _NC_CACHE:
            _NC_CACHE[key] = build(layers, first, layers[-1] == DEPTH - 1)
        nc = _NC_CACHE[key]
        LKEYS = ("vecs", "ssmp", "swT", "sgb", "cw", "wada", "win", "wco", "wso", "wss", "wglu", "wo", "wup", "wdn")
        l0, l1 = layers[0], layers[-1] + 1
        launch_maps = []
        for core in range(ncores):
            m = dict(per_core[core])
            for kk in LKEYS:
                m[kk] = np.ascontiguousarray(sh[kk][l0:l1])
            launch_maps.append(m)
        res = run_bass_kernel_spmd(nc, launch_maps, core_ids=list(range(ncores)))
        if not FUSED:
            for core in range(ncores):
                per_core[core]["xT"] = res.results[core]["xo"]
                per_core[core]["ctxT"] = res.results[core]["co"]
    out = np.zeros((4, T_LAT, D), np.float32)
    for b in range(4):
        out[b] = res.results[b]["xo"].reshape(D, T_LAT).T
    return out
```
